# Optimizing a Trainium2 kernel written in Bass

```python
import jax, jax.numpy as jnp
from jax import lax
import numpy as np

D_MODEL = 1024
BATCH = 2
SEQ = 8192
DEPTH = 4
DEC_BATCH = 16
DEC_SEQ = 64
PAST_LEN = 1024

CHUNK = 64
D_MIX = D_MODEL
D_CONV = D_MIX // 2
D_SGU = D_MIX - D_CONV
N_SGU_HEADS = 8
SGU_HEAD_DIM = D_SGU // N_SGU_HEADS
SGU_CHUNK = 128
CONV_WIDTH = 31
CONV_CTX = CONV_WIDTH - 1
D_FF = 2816
ALPHA = float((2 * DEPTH) ** 0.25)
BETA = float((8 * DEPTH) ** -0.25)
LN_EPS = 1e-5

kernel_name = "hybrid_conv_sgu_streaming_encoder_step"


def layer_norm(x, g, b):
    xf = x.astype(jnp.float32)
    mu = jnp.mean(xf, axis=-1, keepdims=True)
    var = jnp.mean(jnp.square(xf - mu), axis=-1, keepdims=True)
    y = (xf - mu) * lax.rsqrt(var + LN_EPS) * g.astype(jnp.float32) + b.astype(jnp.float32)
    return y.astype(x.dtype)


def swiglu(x, wg, wu, wd):
    return (jax.nn.silu(x @ wg) * (x @ wu)) @ wd


def depthwise_causal(buf, k, bias):
    c = buf.shape[-1]
    out = lax.conv_general_dilated(buf, k[:, None, :], window_strides=(1,), padding='VALID',
                                   dimension_numbers=('NWC', 'WIO', 'NWC'), feature_group_count=c)
    return out + bias


def sgu_mix(v, w_s, b_s):
    bsz, t, _ = v.shape
    L = min(t, SGU_CHUNK)
    vh = v.reshape(bsz, t // L, L, N_SGU_HEADS, SGU_HEAD_DIM)
    w = jnp.tril(w_s[:, :L, :L])
    out = jnp.einsum('hts,bnshd->bnthd', w, vh) + b_s[:, :L].T[None, None, :, :, None]
    return out.reshape(bsz, t, D_SGU)


def trunk_layer(x, conv_ctx, w_ffn1_gate, w_ffn1_up, w_ffn1_down, ln1_g, ln1_b, w_in, conv_k, conv_b,
                conv_ln_g, conv_ln_b, sgu_ln_g, sgu_ln_b, w_sgu, b_sgu, w_out, ln2_g, ln2_b,
                w_ffn2_gate, w_ffn2_up, w_ffn2_down, ln3_g, ln3_b):
    x = layer_norm(ALPHA * x + 0.5 * swiglu(x, w_ffn1_gate, w_ffn1_up, w_ffn1_down), ln1_g, ln1_b)
    h = x @ w_in
    a_val = h[..., :D_CONV]
    a_gate = h[..., D_CONV:2 * D_CONV]
    z = jax.nn.gelu(h[..., 2 * D_CONV:], approximate=False)
    a = a_val * jax.nn.sigmoid(a_gate)
    buf = jnp.concatenate([conv_ctx.astype(a.dtype), a], axis=1)
    c = jax.nn.silu(layer_norm(depthwise_causal(buf, conv_k, conv_b), conv_ln_g, conv_ln_b))
    u = z[..., :D_SGU]
    v = layer_norm(z[..., D_SGU:], sgu_ln_g, sgu_ln_b)
    s = u * sgu_mix(v, w_sgu, b_sgu)
    mix = jnp.concatenate([c, s], axis=-1) @ w_out
    x = layer_norm(ALPHA * x + mix, ln2_g, ln2_b)
    x = layer_norm(ALPHA * x + 0.5 * swiglu(x, w_ffn2_gate, w_ffn2_up, w_ffn2_down), ln3_g, ln3_b)
    return x, buf[:, -CONV_CTX:], v


def setup_inputs(seed: int = 0) -> dict:
    key = jax.random.key(seed)
    ks = jax.random.split(key, 32)
    f32 = jnp.float32
    n = lambda k, shape, s: (jax.random.normal(k, shape, f32) * s)
    L = DEPTH
    return {
        "x_prompt": n(ks[0], (BATCH, SEQ, D_MODEL), 1.0),
        "x_sample": n(ks[1], (DEC_BATCH, DEC_SEQ, D_MODEL), 1.0),
        "cache_conv": n(ks[2], (L, DEC_BATCH, CONV_CTX, D_CONV), 0.5),
        "w_ffn1_gate": n(ks[3], (L, D_MODEL, D_FF), D_MODEL ** -0.5),
        "w_ffn1_up": n(ks[4], (L, D_MODEL, D_FF), D_MODEL ** -0.5),
        "w_ffn1_down": n(ks[5], (L, D_FF, D_MODEL), BETA * D_FF ** -0.5),
        "ln1_g": 1.0 + n(ks[6], (L, D_MODEL), 0.01),
        "ln1_b": n(ks[7], (L, D_MODEL), 0.01),
        "w_in": n(ks[8], (L, D_MODEL, 2 * D_CONV + 2 * D_SGU), D_MODEL ** -0.5),
        "conv_k": n(ks[9], (L, CONV_WIDTH, D_CONV), CONV_WIDTH ** -0.5),
        "conv_b": n(ks[10], (L, D_CONV), 0.01),
        "conv_ln_g": 1.0 + n(ks[11], (L, D_CONV), 0.01),
        "conv_ln_b": n(ks[12], (L, D_CONV), 0.01),
        "sgu_ln_g": 1.0 + n(ks[13], (L, D_SGU), 0.01),
        "sgu_ln_b": n(ks[14], (L, D_SGU), 0.01),
        "w_sgu": n(ks[15], (L, N_SGU_HEADS, SGU_CHUNK, SGU_CHUNK), SGU_CHUNK ** -0.5),
        "b_sgu": 1.0 + n(ks[16], (L, N_SGU_HEADS, SGU_CHUNK), 0.01),
        "w_out": n(ks[17], (L, D_MIX, D_MODEL), BETA * D_MIX ** -0.5),
        "ln2_g": 1.0 + n(ks[18], (L, D_MODEL), 0.01),
        "ln2_b": n(ks[19], (L, D_MODEL), 0.01),
        "w_ffn2_gate": n(ks[20], (L, D_MODEL, D_FF), D_MODEL ** -0.5),
        "w_ffn2_up": n(ks[21], (L, D_MODEL, D_FF), D_MODEL ** -0.5),
        "w_ffn2_down": n(ks[22], (L, D_FF, D_MODEL), BETA * D_FF ** -0.5),
        "ln3_g": 1.0 + n(ks[23], (L, D_MODEL), 0.01),
        "ln3_b": n(ks[24], (L, D_MODEL), 0.01),
    }


def reference(x_prompt, x_sample, cache_conv, w_ffn1_gate, w_ffn1_up, w_ffn1_down, ln1_g, ln1_b, w_in,
              conv_k, conv_b, conv_ln_g, conv_ln_b, sgu_ln_g, sgu_ln_b, w_sgu, b_sgu, w_out, ln2_g, ln2_b,
              w_ffn2_gate, w_ffn2_up, w_ffn2_down, ln3_g, ln3_b):
    xp = x_prompt
    xs = x_sample
    conv_p, conv_s, sgu_s = [], [], []
    for l in range(DEPTH):
        params = (w_ffn1_gate[l], w_ffn1_up[l], w_ffn1_down[l], ln1_g[l], ln1_b[l], w_in[l], conv_k[l],
                  conv_b[l], conv_ln_g[l], conv_ln_b[l], sgu_ln_g[l], sgu_ln_b[l], w_sgu[l], b_sgu[l],
                  w_out[l], ln2_g[l], ln2_b[l], w_ffn2_gate[l], w_ffn2_up[l], w_ffn2_down[l],
                  ln3_g[l], ln3_b[l])
        zero_ctx = jnp.zeros((xp.shape[0], CONV_CTX, D_CONV), xp.dtype)
        xp, cp, _ = trunk_layer(xp, zero_ctx, *params)
        xs, cs, vs = trunk_layer(xs, cache_conv[l], *params)
        conv_p.append(cp)
        conv_s.append(cs)
        sgu_s.append(vs)
    state_conv_prompt = jnp.stack(conv_p, axis=0)
    state_conv_sample = jnp.stack(conv_s, axis=0)
    state_sgu_v_sample = jnp.stack(sgu_s, axis=0)
    return (xp, xs, state_conv_prompt, state_conv_sample, state_sgu_v_sample)
```

```python
import numpy as np
import concourse.bass as bass
import concourse.mybir as mybir
from concourse.bass_utils import run_bass_kernel_spmd
from contextlib import ExitStack

F32 = mybir.dt.float32
BF16 = mybir.dt.bfloat16
AF = mybir.ActivationFunctionType
ALU = mybir.AluOpType

D = 1024
DFF = 2816
NF = 22
DEPTH = 4
DC = 512
NH = 8
HD = 64
KW = 31
CTXN = 30
NB = 2
NT = 9
NCORES = 8
OWN = 2048
ALPHA = float((2 * DEPTH) ** 0.25)
EPS = 1e-5
FG = 4
FGROUPS = [(0, 4), (4, 4), (8, 4), (12, 4), (16, 2), (18, 4)]
NSLOT = 3
SLOTW = 12288


class Buf:
    __slots__ = ("name", "w", "r")

    def __init__(self, name):
        self.name = name
        self.w = None
        self.r = []


class Sched:
    COMPUTE = ("pe", "act", "dve", "pool")

    def __init__(self, nc, es):
        self.nc = nc
        self.es = es
        self.eng = {"pe": nc.tensor, "act": nc.scalar, "dve": nc.vector, "pool": nc.gpsimd, "sp": nc.sync}
        self.sems = {}
        self.cnt = {}
        self.isdma = {}
        for e in self.COMPUTE:
            self.sems[e] = es.enter_context(nc.semaphore("tk_" + e))
            self.cnt[e] = 0
            self.isdma[e] = False
        self.prog = {e: [] for e in self.eng}
        self.waited = {e: {} for e in self.eng}

    def dma_sem(self, name):
        if name not in self.sems:
            self.sems[name] = self.es.enter_context(self.nc.semaphore("dq_" + name))
            self.cnt[name] = 0
            self.isdma[name] = True
        return name

    def _deps(self, eng, reads, writes):
        deps = {}

        def add(ev):
            if ev is None:
                return
            k, v = ev
            if deps.get(k, 0) < v:
                deps[k] = v
        for b in reads:
            add(b.w)
        for b in writes:
            add(b.w)
            for r in b.r:
                add(r)
        waits = []
        for k, v in deps.items():
            if k == "pe" and eng == "pe":
                continue
            if self.isdma[k]:
                v = self.cnt[k]
            if self.waited[eng].get(k, 0) >= v:
                continue
            self.waited[eng][k] = v
            waits.append((k, v))
        return waits

    def _commit(self, ev, reads, writes):
        for b in reads:
            b.r.append(ev)
            if len(b.r) > 64:
                m = {}
                for k, v in b.r:
                    if m.get(k, 0) < v:
                        m[k] = v
                b.r = list(m.items())
        for b in writes:
            b.w = ev
            b.r = []

    def op(self, eng, fn, reads=(), writes=()):
        waits = self._deps(eng, reads, writes)
        self.cnt[eng] += 1
        ev = (eng, self.cnt[eng])
        self._commit(ev, reads, writes)
        self.prog[eng].append((waits, fn, (eng, 1)))
        return ev

    def dma(self, q, out, in_, reads=(), writes=(), sem="misc", **kw):
        self.dma_sem(sem)
        waits = self._deps(q, reads, writes)
        self.cnt[sem] += 16
        ev = (sem, self.cnt[sem])
        self._commit(ev, reads, writes)
        self.prog[q].append((waits, lambda e: e.dma_start(out=out, in_=in_, **kw), (sem, 16)))
        return ev

    def wait_all(self, eng, semnames):
        waits = [(k, self.cnt[k]) for k in semnames if self.cnt.get(k, 0) > 0]
        self.prog[eng].append((waits, None, None))

    def emit(self, block):
        def run(engname):
            def body(e):
                for waits, fn, inc in self.prog[engname]:
                    for k, v in waits:
                        e.wait_ge(self.sems[k], v)
                    if fn is not None:
                        ins = fn(e)
                        ins.then_inc(self.sems[inc[0]], inc[1])
            return body
        block.tensor(run("pe"))
        block.scalar(run("act"))
        block.vector(run("dve"))
        block.gpsimd(run("pool"))
        block.sync(run("sp"))


def build_program(NL=DEPTH, dbg_stage=None):
    nc = bass.Bass("TRN2", target_bir_lowering=False)
    TOK = NB * NT * 128

    def din(name, shape):
        return nc.dram_tensor(name, list(shape), F32, kind="ExternalInput").ap()

    def dout(name, shape):
        return nc.dram_tensor(name, list(shape), F32, kind="ExternalOutput").ap()

    xin = din("xin", [TOK, D])
    cmask_d = din("cmask", [128, 1])
    cache_d = din("cache", [DEPTH, 2, CTXN, DC])
    wd = {}
    for nm, shp in [("w_ffn1_gate", [DEPTH, D, DFF]), ("w_ffn1_up", [DEPTH, D, DFF]), ("w_ffn1_down", [DEPTH, DFF, D]),
                    ("w_ffn2_gate", [DEPTH, D, DFF]), ("w_ffn2_up", [DEPTH, D, DFF]), ("w_ffn2_down", [DEPTH, DFF, D]),
                    ("w_in", [DEPTH, D, 2048]), ("w_out", [DEPTH, D, D]),
                    ("ln1_g", [DEPTH, D]), ("ln1_b", [DEPTH, D]), ("ln2_g", [DEPTH, D]), ("ln2_b", [DEPTH, D]),
                    ("ln3_g", [DEPTH, D]), ("ln3_b", [DEPTH, D]),
                    ("conv_k", [DEPTH, KW, DC]), ("conv_b", [DEPTH, DC]), ("conv_ln_g", [DEPTH, DC]),
                    ("conv_ln_b", [DEPTH, DC]), ("sgu_ln_g", [DEPTH, DC]), ("sgu_ln_b", [DEPTH, DC]),
                    ("w_sgu", [DEPTH, NH, 128, 128]), ("b_sgu", [DEPTH, NH, 128])]:
        wd[nm] = din(nm, shp)
    y_d = dout("y", [TOK - 128, D])
    convp_d = dout("conv_p", [DEPTH, CTXN, DC])
    convs_d = dout("conv_s", [DEPTH, 2, CTXN, DC])
    sguv_d = dout("sgu_v", [DEPTH, 128, DC])
    dbg_d = dout("dbg", [TOK, D]) if dbg_stage else None

    es = ExitStack()
    with es:
        es.enter_context(nc.allow_non_contiguous_dma(reason="small strided parameter loads"))
        S = Sched(nc, es)

        def sb(name, shape, dt=F32):
            return es.enter_context(nc.sbuf_tensor(name, list(shape), dt))

        X = sb("X", [128, NT, D])
        XT = sb("XT", [128, 8, NT * 128], BF16)
        RING = [sb("ring%d" % i, [128, SLOTW], BF16) for i in range(NSLOT)]
        GT = [sb("gt%d" % i, [128, FG, 384], BF16) for i in range(2)]
        SIL = [sb("sil%d" % i, [128, 384]) for i in range(2)]
        XN = [sb("xn%d" % i, [128, D], BF16) for i in range(2)]
        LNG = sb("lng", [128, 2, D])
        LNS = [dict(st=sb("lnst%d" % i, [128, 3, 2, 6]), mv=sb("lnmv%d" % i, [128, 3, 2]),
                    rstd=sb("lnrs%d" % i, [128, 3]), nmr=sb("lnnm%d" % i, [128, 3])) for i in range(2)]
        ABF = sb("abf", [128, 4, CTXN + 384], BF16)
        A30 = sb("a30", [128, 4, 2, CTXN])
        D32 = sb("d32", [128, KW, 4, 32], BF16)
        M32 = sb("m32", [128, 32])
        ACC = sb("acc", [128, 4, 384])
        SQ0 = sb("sq0", [128, 384])
        SQ = [SQ0, SQ0]
        CT = sb("ct", [128, 4, 384], BF16)
        STt = sb("stt", [128, 4, 384], BF16)
        U = [sb("u%d" % i, [128, DC]) for i in range(3)]
        ZV = [sb("zv%d" % i, [128, DC]) for i in range(3)]
        VB = [sb("vb%d" % i, [128, DC], BF16) for i in range(3)]
        SS = VB
        MEANB = GT[0][:].rearrange("p f n -> p (f n)").bitcast(F32)[:, 0:384]
        RSTDB = GT[1][:].rearrange("p f n -> p (f n)").bitcast(F32)[:, 0:384]
        SGB = sb("sgb", [128, 2, DC])
        WT = sb("wt", [128, NH, 128], BF16)
        WTS = sb("wts", [128, NH, 128], BF16)
        WNAT = sb("wnat", [128, NH, 128], BF16)
        WNATS = sb("wnats", [128, NH, 128], BF16)
        BIAS = sb("bias", [128, NH])
        BIASS = sb("biass", [128, NH])
        MASKB = sb("maskb", [128, 128], BF16)
        SCR1 = sb("scr1", [124, DC])
        KNAT = SCR1[0:KW, :]
        CNAT = SCR1[64:64 + 2 * CTXN, :]
        KCOL = sb("kcol", [128, 4, 32])
        CB = sb("cb", [128, 4])
        CLG = sb("clg", [128, 4])
        CLB = sb("clb", [128, 4])
        CTX = sb("ctx", [128, DEPTH, 4, CTXN], BF16)
        CTXS = sb("ctxs", [128, 4, 2, CTXN], BF16)
        CMASK = sb("cmaskt", [128, 1])
        EPS1 = sb("eps1", [128, 1])
        EPS4 = sb("eps4", [128, 1])
        IDF = sb("idf", [128, 128])
        IDB = sb("idb", [128, 128], BF16)
        ONESC = sb("onesc", [128, 128])
        STG = U[2][0:CTXN, :]
        PS = es.enter_context(nc.psum_tensor("psall", [128, 8, 512], F32))

        def psbf(bank):
            return PS[:, bank, :].bitcast(BF16)

        bX = [Buf("X%d" % t) for t in range(NT)]
        bXT = [Buf("XT%d" % t) for t in range(NT)]
        bRING = [Buf("ring%d" % i) for i in range(NSLOT)]
        bGT = [Buf("gt%d" % i) for i in range(2)]
        bSIL = [Buf("sil%d" % i) for i in range(2)]
        bXN = [Buf("xn%d" % i) for i in range(2)]
        bLNG = Buf("lng")
        bLNS = [Buf("lns%d" % i) for i in range(2)]
        bLNSr = [Buf("lnsr%d" % i) for i in range(2)]
        bABUF = Buf("abuf")
        bACC = [Buf("acc%d" % c) for c in range(4)]
        bSQ0 = Buf("sq0")
        bSQ = [bSQ0, bSQ0]
        bCT = Buf("ct")
        bST = Buf("stt")
        bU = [Buf("u%d" % i) for i in range(3)]
        bZV = [Buf("zv%d" % i) for i in range(3)]
        bVB = [Buf("vb%d" % i) for i in range(3)]
        bSS = bVB
        bPAR = Buf("layer_params")
        bWNAT = Buf("wnat")
        bKNAT = Buf("knat")
        bCNAT = Buf("cnat")
        bCTX = [Buf("ctx%d" % l) for l in range(DEPTH)]
        bCONST = Buf("const")
        bSTG = bU[2]
        bA30 = Buf("a30")
        bPS = [Buf("ps%d" % i) for i in range(8)]
        bPS7h = [bPS[7]]

        def bb(bank):
            return [bPS[bank]]

        S.op("pool", lambda e: e.memset(IDF[:], 0.0), writes=[bCONST])
        S.op("pool", lambda e: e.affine_select(out=IDF[:], in_=IDF[:], pattern=[[-1, 128]], compare_op=ALU.not_equal,
                                               fill=1.0, base=0, channel_multiplier=1), reads=[bCONST], writes=[bCONST])
        S.op("pool", lambda e: e.memset(ONESC[:], 1.0), writes=[bCONST])
        S.op("pool", lambda e: e.affine_select(out=ONESC[:], in_=ONESC[:], pattern=[[1, 128]], compare_op=ALU.is_ge,
                                               fill=0.0, base=0, channel_multiplier=-1), reads=[bCONST], writes=[bCONST])

        def c_misc(e):
            e.memset(EPS1[:], EPS)
            e.memset(EPS4[:], 4.0 * EPS)
            e.memset(CTX[:], 0.0)
            e.memset(KCOL[:], 0.0)
            return e.memset(WNATS[:], 0.0)
        S.op("pool", c_misc, writes=[bCONST, bWNAT, bPAR] + bCTX)
        S.op("act", lambda e: e.copy(out=IDB[:], in_=IDF[:]), reads=[bCONST], writes=[bCONST])
        S.op("act", lambda e: e.copy(out=MASKB[:], in_=ONESC[:]), reads=[bCONST], writes=[bCONST])
        S.op("dve", lambda e: e.memset(ONESC[:], 1.0 / DC), reads=[bCONST], writes=[bCONST])
        S.op("dve", lambda e: e.tensor_tensor(out=M32[:], in0=IDF[:, 0:32], in1=IDF[:, 32:64], op=ALU.add),
             reads=[bCONST], writes=[bCONST])
        S.op("dve", lambda e: e.tensor_tensor(out=M32[:], in0=M32[:], in1=IDF[:, 64:96], op=ALU.add),
             reads=[bCONST], writes=[bCONST])
        S.op("dve", lambda e: e.tensor_tensor(out=M32[:], in0=M32[:], in1=IDF[:, 96:128], op=ALU.add),
             reads=[bCONST], writes=[bCONST])
        S.dma("sp", CMASK[:], cmask_d, writes=[bCONST], sem="cst")

        units = []
        for b in range(NB):
            for l in range(NL):
                for g in range(len(FGROUPS)):
                    units.append(("F", l, 1, g))
                units.append(("MA", l))
                units.append(("MB", l))
                for g in range(len(FGROUPS)):
                    units.append(("F", l, 2, g))
        ring_state = {"next": 0}

        def load_next():
            u = ring_state["next"]
            if u >= len(units):
                return
            ring_state["next"] = u + 1
            slot = u % NSLOT
            R = RING[slot]
            sem = "ring%d" % slot
            un = units[u]
            if un[0] == "F":
                _, l, which, g = un
                f0, n = FGROUPS[g]
                wg = wd["w_ffn%d_gate" % which]
                wu = wd["w_ffn%d_up" % which]
                wdn = wd["w_ffn%d_down" % which]
                S.dma("pool", R[:, 0:8 * n * 128].rearrange("p (k n) -> p k n", k=8),
                      wg[l, :, f0 * 128:(f0 + n) * 128].rearrange("(k p) n -> p k n", p=128),
                      writes=[bRING[slot]], sem=sem)
                S.dma("pool", R[:, 4096:4096 + 8 * n * 128].rearrange("p (k n) -> p k n", k=8),
                      wu[l, :, f0 * 128:(f0 + n) * 128].rearrange("(k p) n -> p k n", p=128),
                      writes=[bRING[slot]], sem=sem)
                S.dma("pool", R[:, 8192:8192 + n * D].rearrange("p (f n) -> p f n", f=n),
                      wdn[l, f0 * 128:(f0 + n) * 128, :].rearrange("(f p) n -> p f n", p=128),
                      writes=[bRING[slot]], sem=sem)
            else:
                l = un[1]
                c0 = 0 if un[0] == "MA" else 1024
                o0 = 0 if un[0] == "MA" else 512
                S.dma("pool", R[:, 0:8192].rearrange("p (k n) -> p k n", k=8),
                      wd["w_in"][l, :, c0:c0 + 1024].rearrange("(k p) n -> p k n", p=128),
                      writes=[bRING[slot]], sem=sem)
                S.dma("pool", R[:, 8192:12288].rearrange("p (k n) -> p k n", k=8),
                      wd["w_out"][l, :, o0:o0 + 512].rearrange("(k p) n -> p k n", p=128),
                      writes=[bRING[slot]], sem=sem)

        cur_unit = {"i": 0}

        def take_unit():
            u = cur_unit["i"]
            cur_unit["i"] = u + 1
            return u % NSLOT

        rr = {"ln": 0, "trb": 0, "gu": 0, "y": 0, "gtb": 0, "mx": 0, "bt": 0, "sq": 0}
        deferred = {}

        def ensure_xt(tiles):
            for t in tiles:
                fn = deferred.pop(t, None)
                if fn is not None:
                    fn()

        def flush_deferred():
            ensure_xt(sorted(deferred.keys()))

        def transposes_to_XT(t, xn_i, bank):
            pb = psbf(bank)

            def tr(e):
                for k in range(8):
                    ins = e.transpose(out=pb[:, k * 128:(k + 1) * 128], in_=XN[xn_i][:, k * 128:(k + 1) * 128],
                                      identity=IDB[:])
                return ins
            S.op("pe", tr, reads=[bXN[xn_i], bCONST], writes=[bPS[bank]])
            S.op("act", lambda e: e.copy(out=XT[:, :, t * 128:(t + 1) * 128],
                                         in_=pb.rearrange("p (k n) -> p k n", k=8)),
                 reads=[bPS[bank]], writes=[bXT[t]])

        def layer_norm_group(tiles, eps_tile, need_xt=True):
            i = rr["ln"] % 2
            rr["ln"] += 1
            L = LNS[i]
            n = len(tiles)
            for k, t in enumerate(tiles):
                def stats(e, k=k, t=t):
                    e.bn_stats(out=L["st"][:, k, 0, :], in_=X[:, t, 0:512])
                    return e.bn_stats(out=L["st"][:, k, 1, :], in_=X[:, t, 512:1024])
                S.op("dve", stats, reads=[bX[t]], writes=[bLNS[i], bLNSr[i]])
            for k, t in enumerate(tiles):
                S.op("dve", lambda e, k=k: e.bn_aggr(out=L["mv"][:, k, :], in_=L["st"][:, k, :, :]),
                     reads=[bLNS[i]], writes=[bLNS[i]])
            S.op("act", lambda e: e.activation(out=L["rstd"][:, 0:n], in_=L["mv"][:, 0:n, 1], func=AF.Sqrt,
                                               bias=eps_tile[:, 0:1], scale=1.0),
                 reads=[bLNS[i], bCONST], writes=[bLNSr[i]])
            for k, t in enumerate(tiles):
                S.op("dve", lambda e, k=k, t=t: e.scalar_tensor_tensor(
                    out=X[:, t, :], in0=X[:, t, :], scalar=L["mv"][:, k, 0:1], in1=LNG[:, 0, :],
                    op0=ALU.subtract, op1=ALU.mult),
                     reads=[bX[t], bLNS[i], bLNG], writes=[bX[t]])
            S.op("dve", lambda e: e.reciprocal(out=L["rstd"][:, 0:n], in_=L["rstd"][:, 0:n]),
                 reads=[bLNSr[i]], writes=[bLNSr[i]])
            for k, t in enumerate(tiles):
                S.op("dve", lambda e, k=k, t=t: e.scalar_tensor_tensor(
                    out=X[:, t, :], in0=X[:, t, :], scalar=L["rstd"][:, k:k + 1], in1=LNG[:, 1, :],
                    op0=ALU.mult, op1=ALU.add),
                     reads=[bX[t], bLNSr[i], bLNG], writes=[bX[t]])
            if need_xt:
                for t in tiles:
                    def later(t=t):
                        xi = rr["trb"] % 2
                        rr["trb"] += 1
                        S.op("act", lambda e: e.copy(out=XN[xi][:], in_=X[:, t, :]), reads=[bX[t]], writes=[bXN[xi]])
                        transposes_to_XT(t, xi, 0 if xi == 0 else 2)
                    deferred[t] = later

        def load_lng(gname, bname, l):
            S.dma("sp", LNG[:, 0, :], wd[gname][l:l + 1, :].partition_broadcast(128), writes=[bLNG], sem="lng")
            S.dma("sp", LNG[:, 1, :], wd[bname][l:l + 1, :].partition_broadcast(128), writes=[bLNG], sem="lng")

        def ffn(l, which, last_layer):
            load_lng("ln1_g" if which == 1 else "ln3_g", "ln1_b" if which == 1 else "ln3_b", l)
            slots = {}
            sched = [(g, tg) for g in range(len(FGROUPS)) for tg in range(NT // 3)]

            def slot_of(g):
                if g not in slots:
                    slots[g] = take_unit()
                return slots[g]

            gt_of = {}

            def P1(idx):
                g, tg = sched[idx]
                f0, n = FGROUPS[g]
                slot = slot_of(g)
                R = RING[slot]
                Wg = R[:, 0:8 * n * 128].rearrange("p (k n) -> p k n", k=8)
                Wu = R[:, 4096:4096 + 8 * n * 128].rearrange("p (k n) -> p k n", k=8)
                gi = rr["gtb"] % 2
                rr["gtb"] += 1
                gt_of[idx] = gi
                c0 = tg * 384
                ensure_xt([tg * 3 + j for j in range(3)])
                xbufs = [bXT[tg * 3 + j] for j in range(3)]
                for fi in range(n):
                    bp = rr["gu"] % 2
                    rr["gu"] += 1
                    gb, ub = 2 * bp, 2 * bp + 1

                    def mm(e, W=None, bank=0, fi=fi):
                        for k in range(8):
                            ins = e.matmul(PS[:, bank, 0:384], lhsT=W[:, k, fi * 128:(fi + 1) * 128],
                                           rhs=XT[:, k, c0:c0 + 384], start=(k == 0), stop=(k == 7))
                        return ins
                    S.op("pe", lambda e, mm=mm, gb=gb: mm(e, Wg, gb), reads=[bRING[slot]] + xbufs, writes=[bPS[gb]])
                    S.op("pe", lambda e, mm=mm, ub=ub: mm(e, Wu, ub), reads=[bRING[slot]] + xbufs, writes=[bPS[ub]])
                    S.op("act", lambda e, gb=gb, bp=bp: e.activation(out=SIL[bp][:], in_=PS[:, gb, 0:384], func=AF.Silu),
                         reads=[bPS[gb]], writes=[bSIL[bp]])
                    S.op("dve", lambda e, ub=ub, bp=bp, fi=fi, gi=gi: e.tensor_tensor(
                        out=GT[gi][:, fi, :], in0=SIL[bp][:], in1=PS[:, ub, 0:384], op=ALU.mult),
                         reads=[bSIL[bp], bPS[ub]], writes=[bGT[gi]])

            def P2(idx):
                g, tg = sched[idx]
                f0, n = FGROUPS[g]
                slot = slot_of(g)
                R = RING[slot]
                Wd_ = R[:, 8192:8192 + n * D].rearrange("p (f n) -> p f n", f=n)
                gi = gt_of[idx]
                for j in range(3):
                    t = tg * 3 + j
                    yb = 4 + 2 * (rr["y"] % 2)
                    rr["y"] += 1

                    def mm(e, j=j, yb=yb):
                        for half in range(2):
                            for fi in range(n):
                                ins = e.matmul(PS[:, yb + half, :], lhsT=GT[gi][:, fi, j * 128:(j + 1) * 128],
                                               rhs=Wd_[:, fi, half * 512:(half + 1) * 512],
                                               start=(fi == 0), stop=(fi == n - 1))
                        return ins
                    S.op("pe", mm, reads=[bRING[slot], bGT[gi]], writes=bb(yb) + bb(yb + 1))
                    ysrc = PS[:, yb:yb + 2, :].rearrange("p a n -> p (a n)")
                    if g == 0:
                        S.op("dve", lambda e, t=t, ysrc=ysrc: e.scalar_tensor_tensor(
                            out=X[:, t, :], in0=X[:, t, :], scalar=2.0 * ALPHA, in1=ysrc, op0=ALU.mult, op1=ALU.add),
                             reads=[bX[t]] + bb(yb) + bb(yb + 1), writes=[bX[t]])
                    else:
                        S.op("dve", lambda e, t=t, ysrc=ysrc: e.tensor_tensor(
                            out=X[:, t, :], in0=X[:, t, :], in1=ysrc, op=ALU.add),
                             reads=[bX[t]] + bb(yb) + bb(yb + 1), writes=[bX[t]])
                if g == len(FGROUPS) - 1:
                    if tg >= 1:
                        ensure_xt([(tg - 1) * 3 + j for j in range(3)])
                    layer_norm_group([tg * 3 + j for j in range(3)], EPS4, need_xt=not (last_layer and which == 2))

            P1(0)
            for idx in range(len(sched)):
                if idx + 1 < len(sched):
                    P1(idx + 1)
                P2(idx)
                g, tg = sched[idx]
                if g == len(FGROUPS) - 1 and idx + 1 < len(sched):
                    pass
                if tg == NT // 3 - 1:
                    load_next()
            if last_layer and which == 2:
                flush_deferred()

        def prep_dma(l):
            S.dma("sp", KNAT, wd["conv_k"][l], writes=[bKNAT], sem="par")
            S.dma("sp", CNAT, cache_d[l].rearrange("b r c -> (b r) c"), writes=[bCNAT], sem="par")
            S.dma("sp", CB[:], wd["conv_b"][l].rearrange("(c p) -> p c", p=128), writes=[bPAR], sem="par")
            S.dma("sp", CLG[:], wd["conv_ln_g"][l].rearrange("(c p) -> p c", p=128), writes=[bPAR], sem="par")
            S.dma("sp", CLB[:], wd["conv_ln_b"][l].rearrange("(c p) -> p c", p=128), writes=[bPAR], sem="par")
            S.dma("sp", SGB[:, 0, :], wd["sgu_ln_g"][l:l + 1, :].partition_broadcast(128), writes=[bPAR], sem="par")
            S.dma("sp", SGB[:, 1, :], wd["sgu_ln_b"][l:l + 1, :].partition_broadcast(128), writes=[bPAR], sem="par")
            S.dma("sp", BIAS[:], wd["b_sgu"][l].rearrange("h t -> t h"), writes=[bPAR], sem="par")
            S.dma("sp", BIASS[0:64, :], wd["b_sgu"][l, :, 0:64].rearrange("h t -> t h"), writes=[bPAR], sem="par")
            S.dma("sp", BIASS[64:128, :], wd["b_sgu"][l, :, 0:64].rearrange("h t -> t h"), writes=[bPAR], sem="par")
            S.dma("pool", WNAT[:], wd["w_sgu"][l].rearrange("h t s -> t h s"), writes=[bWNAT], sem="wn")
            S.dma("pool", WNATS[0:64, :, 0:64], wd["w_sgu"][l, :, 0:64, 0:64].rearrange("h t s -> t h s"),
                  writes=[bWNAT], sem="wn")
            S.dma("pool", WNATS[64:128, :, 64:128], wd["w_sgu"][l, :, 0:64, 0:64].rearrange("h t s -> t h s"),
                  writes=[bWNAT], sem="wn")

        def prep_pe(l):
            pk = PS[:, 7, 0:128].rearrange("p (c j) -> p c j", c=4)

            def trk(e):
                for c in range(4):
                    ins = e.transpose(out=pk[:, c, 0:KW], in_=SCR1[0:KW, c * 128:(c + 1) * 128],
                                      identity=IDF[0:KW, 0:KW])
                return ins
            S.op("pe", trk, reads=[bKNAT, bCONST], writes=bPS7h)
            S.op("act", lambda e: e.copy(out=KCOL[:, :, 0:KW], in_=pk[:, :, 0:KW]), reads=bPS7h, writes=[bPAR])
            for c in range(4):
                S.op("dve", lambda e, c=c: e.scalar_tensor_tensor(
                    out=D32[:, :, c, :], in0=M32[:].unsqueeze(1).to_broadcast([128, KW, 32]), scalar=0.5,
                    in1=KCOL[:, c, 0:KW].unsqueeze(2).to_broadcast([128, KW, 32]), op0=ALU.mult, op1=ALU.mult),
                     reads=[bPAR, bCONST], writes=[bPAR])
            pc = PS[:, 7, 0:256].rearrange("p (c j) -> p c j", c=4)

            def trc(e):
                for c in range(4):
                    ins = e.transpose(out=pc[:, c, 0:2 * CTXN], in_=SCR1[64:64 + 2 * CTXN, c * 128:(c + 1) * 128],
                                      identity=IDF[64:64 + 2 * CTXN, 64:64 + 2 * CTXN])
                return ins
            S.op("pe", trc, reads=[bCNAT, bCONST], writes=bPS7h)
            S.op("act", lambda e: e.activation(out=CTXS[:].rearrange("p c s j -> p c (s j)"), in_=pc[:, :, 0:2 * CTXN],
                                               func=AF.Identity, scale=2.0),
                 reads=bPS7h, writes=[bPAR])
            for (src, dst) in ((WNAT, WT), (WNATS, WTS)):
                pb = psbf(7)

                def trw(e, src=src):
                    for h in range(NH):
                        ins = e.transpose(out=pb[:, h * 128:(h + 1) * 128], in_=src[:, h, :], identity=IDB[:])
                    return ins
                S.op("pe", trw, reads=[bWNAT, bCONST], writes=bPS7h)
                S.op("dve", lambda e, dst=dst, pb=pb: e.tensor_tensor(
                    out=dst[:], in0=pb.rearrange("p (h t) -> p h t", h=NH),
                    in1=MASKB[:].unsqueeze(1).to_broadcast([128, NH, 128]), op=ALU.mult),
                     reads=bPS7h + [bCONST], writes=[bPAR])

        def mixer(l, b):
            load_lng("ln2_g", "ln2_b", l)
            sA = take_unit()
            sB = take_unit()
            WinA = RING[sA][:, 0:8192].rearrange("p (k n) -> p k n", k=8)
            WoutA = RING[sA][:, 8192:12288].rearrange("p (k n) -> p k n", k=8)
            WinB = RING[sB][:, 0:8192].rearrange("p (k n) -> p k n", k=8)
            WoutB = RING[sB][:, 8192:12288].rearrange("p (k n) -> p k n", k=8)
            if b == 0:
                groups = [(0, 3, False), (3, 3, False), (6, 3, False)]
            else:
                groups = [(0, 3, False), (3, 3, False), (6, 2, False), (8, 1, True)]
            AS = ABF[:, :, 0:188].rearrange("p c (s j) -> p c s j", s=2)

            def partA(gi, t0, n, is_s, between=None):
                N = n * 128
                c0 = t0 * 128
                ensure_xt([t0 + j for j in range(n)])
                xbufs = [bXT[t0 + j] for j in range(n)]
                want_state = is_s or (b == NB - 1 and gi == len(groups) - 2)
                if is_s:
                    S.op("act", lambda e: e.copy(out=AS[:, :, :, 0:CTXN], in_=CTXS[:]), reads=[bPAR], writes=[bABUF])
                else:
                    S.op("act", lambda e: e.copy(out=ABF[:, :, 0:CTXN], in_=CTX[:, l, :, :]),
                         reads=[bCTX[l]], writes=[bABUF])
                for c in range(4):
                    vb, gb = (2 * c) % 4, (2 * c + 1) % 4

                    def mm(e, col0, bank):
                        for k in range(8):
                            ins = e.matmul(PS[:, bank, 0:N], lhsT=WinA[:, k, col0:col0 + 128],
                                           rhs=XT[:, k, c0:c0 + N], start=(k == 0), stop=(k == 7))
                        return ins
                    S.op("pe", lambda e, mm=mm, c=c, vb=vb: mm(e, c * 128, vb), reads=[bRING[sA]] + xbufs, writes=[bPS[vb]])
                    S.op("pe", lambda e, mm=mm, c=c, gb=gb: mm(e, 512 + c * 128, gb), reads=[bRING[sA]] + xbufs,
                         writes=[bPS[gb]])
                    si = c % 2
                    S.op("act", lambda e, gb=gb, si=si: e.activation(out=SIL[si][:, 0:N], in_=PS[:, gb, 0:N],
                                                                     func=AF.Tanh, scale=0.5),
                         reads=[bPS[gb]], writes=[bSIL[si]])
                    if is_s:
                        S.op("dve", lambda e, c=c, vb=vb, si=si: e.scalar_tensor_tensor(
                            out=AS[:, c, :, CTXN:CTXN + 64],
                            in0=SIL[si][:, 0:128].rearrange("p (s j) -> p s j", s=2), scalar=1.0,
                            in1=PS[:, vb, 0:128].rearrange("p (s j) -> p s j", s=2), op0=ALU.add, op1=ALU.mult),
                             reads=[bPS[vb], bSIL[si]], writes=[bABUF])
                        S.op("dve", lambda e, c=c, vb=vb, si=si: e.scalar_tensor_tensor(
                            out=A30[:, c, :, :],
                            in0=SIL[si][:, 0:128].rearrange("p (s j) -> p s j", s=2)[:, :, 34:64], scalar=1.0,
                            in1=PS[:, vb, 0:128].rearrange("p (s j) -> p s j", s=2)[:, :, 34:64],
                            op0=ALU.add, op1=ALU.mult),
                             reads=[bPS[vb], bSIL[si]], writes=[bA30])
                    else:
                        first_halo = (b == 0 and gi == 0)

                        S.op("dve", lambda e, c=c, vb=vb, si=si: e.scalar_tensor_tensor(
                            out=ABF[:, c, CTXN:CTXN + N], in0=SIL[si][:, 0:N], scalar=1.0, in1=PS[:, vb, 0:N],
                            op0=ALU.add, op1=ALU.mult),
                             reads=[bPS[vb], bSIL[si]], writes=[bABUF])
                        if first_halo:
                            S.op("dve", lambda e, c=c: e.tensor_scalar(
                                out=ABF[:, c, CTXN:CTXN + 128], in0=ABF[:, c, CTXN:CTXN + 128], scalar1=CMASK[:, 0:1],
                                scalar2=None, op0=ALU.mult),
                                 reads=[bABUF, bCONST], writes=[bABUF])
                        if want_state:
                            S.op("dve", lambda e, c=c, vb=vb, si=si: e.scalar_tensor_tensor(
                                out=A30[:, c, 0, :], in0=SIL[si][:, N - CTXN:N], scalar=1.0, in1=PS[:, vb, N - CTXN:N],
                                op0=ALU.add, op1=ALU.mult),
                                 reads=[bPS[vb], bSIL[si]], writes=[bA30])
                    if between is not None:
                        between(c)
                if not is_s:
                    S.op("act", lambda e: e.copy(out=CTX[:, l, :, :], in_=ABF[:, :, N:N + CTXN]),
                         reads=[bABUF], writes=[bCTX[l]])
                if want_state:
                    nseq = 2 if is_s else 1
                    for s_ in range(nseq):
                        def trs(e, s_=s_):
                            for c in range(4):
                                ins = e.transpose(out=PS[0:CTXN, 7, c * 128:(c + 1) * 128], in_=A30[:, c, s_, :],
                                                  identity=IDF[:])
                            return ins
                        S.op("pe", trs, reads=[bA30, bCONST], writes=bPS7h)
                        S.op("act", lambda e: e.activation(out=STG, in_=PS[0:CTXN, 7, :], func=AF.Identity, scale=0.5),
                             reads=bPS7h, writes=[bSTG])
                        dst = convs_d[l, s_] if is_s else convp_d[l]
                        S.dma("sp", dst, STG, reads=[bSTG], sem="sout")

            def partB(gi, t0, n, is_s):
                li = rr["ln"] % 2
                rr["ln"] += 1
                L = LNS[li]
                for j in range(n):
                    t = t0 + j
                    hb = 4 + 2 * (j % 2)

                    def mmb(e, t=t, hb=hb):
                        for half in range(2):
                            for k in range(8):
                                ins = e.matmul(PS[:, hb + half, :], lhsT=XT[:, k, t * 128:(t + 1) * 128],
                                               rhs=WinB[:, k, half * 512:(half + 1) * 512],
                                               start=(k == 0), stop=(k == 7))
                        return ins
                    wr = bb(hb) + bb(hb + 1)
                    S.op("pe", mmb, reads=[bRING[sB], bXT[t]], writes=wr)
                    S.op("act", lambda e, j=j, hb=hb: e.activation(out=U[j][:], in_=PS[:, hb, :], func=AF.Gelu),
                         reads=[bPS[hb]], writes=[bU[j]])
                    S.op("act", lambda e, j=j, hb=hb: e.activation(out=ZV[j][:], in_=PS[:, hb + 1, :], func=AF.Gelu),
                         reads=bb(hb + 1), writes=[bZV[j]])
                for j in range(n):
                    S.op("dve", lambda e, j=j: e.bn_stats(out=L["st"][:, j, 0, :], in_=ZV[j][:]),
                         reads=[bZV[j]], writes=[bLNS[li], bLNSr[li]])
                for j in range(n):
                    S.op("dve", lambda e, j=j: e.bn_aggr(out=L["mv"][:, j, :], in_=L["st"][:, j, 0:1, :]),
                         reads=[bLNS[li]], writes=[bLNS[li]])
                S.op("act", lambda e: e.activation(out=L["rstd"][:, 0:n], in_=L["mv"][:, 0:n, 1], func=AF.Sqrt,
                                                   bias=EPS1[:, 0:1], scale=1.0),
                     reads=[bLNS[li], bCONST], writes=[bLNSr[li]])
                for j in range(n):
                    S.op("dve", lambda e, j=j: e.scalar_tensor_tensor(
                        out=ZV[j][:], in0=ZV[j][:], scalar=L["mv"][:, j, 0:1], in1=SGB[:, 0, :],
                        op0=ALU.subtract, op1=ALU.mult),
                         reads=[bZV[j], bLNS[li], bPAR], writes=[bZV[j]])
                S.op("dve", lambda e: e.reciprocal(out=L["rstd"][:, 0:n], in_=L["rstd"][:, 0:n]),
                     reads=[bLNSr[li]], writes=[bLNSr[li]])
                for j in range(n):
                    if is_s:
                        S.op("dve", lambda e, j=j: e.scalar_tensor_tensor(
                            out=ZV[j][:], in0=ZV[j][:], scalar=L["rstd"][:, j:j + 1], in1=SGB[:, 1, :],
                            op0=ALU.mult, op1=ALU.add),
                             reads=[bZV[j], bLNSr[li], bPAR], writes=[bZV[j]])
                        S.op("act", lambda e, j=j: e.copy(out=VB[j][:], in_=ZV[j][:]), reads=[bZV[j]], writes=[bVB[j]])
                        S.dma("sp", sguv_d[l], ZV[j][:], reads=[bZV[j]], sem="sout")
                    else:
                        S.op("dve", lambda e, j=j: e.scalar_tensor_tensor(
                            out=VB[j][:], in0=ZV[j][:], scalar=L["rstd"][:, j:j + 1], in1=SGB[:, 1, :],
                            op0=ALU.mult, op1=ALU.add),
                             reads=[bZV[j], bLNSr[li], bPAR], writes=[bVB[j]])

            def partSGU(gi, t0, n, is_s):
                Wm = WTS if is_s else WT
                Bm = BIASS if is_s else BIAS
                mbank = [4, 5, 6]
                for j in range(n):
                    def mms(e, j=j):
                        for h in range(NH):
                            ins = e.matmul(PS[:, mbank[j], h * HD:(h + 1) * HD], lhsT=Wm[:, h, :],
                                           rhs=VB[j][:, h * HD:(h + 1) * HD], start=True, stop=True)
                        return ins
                    S.op("pe", mms, reads=[bPAR, bVB[j]], writes=[bPS[mbank[j]]])
                for j in range(n):
                    S.op("dve", lambda e, j=j: e.tensor_tensor(
                        out=ZV[j][:].rearrange("p (h d) -> p h d", h=NH),
                        in0=PS[:, mbank[j], :].rearrange("p (h d) -> p h d", h=NH),
                        in1=Bm[:].unsqueeze(2).to_broadcast([128, NH, HD]), op=ALU.add),
                         reads=[bPS[mbank[j]], bPAR], writes=[bZV[j]])
                    S.op("dve", lambda e, j=j: e.tensor_tensor(out=SS[j][:], in0=ZV[j][:], in1=U[j][:], op=ALU.mult),
                         reads=[bZV[j], bU[j]], writes=[bSS[j]])
                for j in range(n):
                    pb = psbf(mbank[j])

                    def trss(e, j=j, pb=pb):
                        for c in range(4):
                            ins = e.transpose(out=pb[:, c * 128:(c + 1) * 128],
                                              in_=SS[j][:, c * 128:(c + 1) * 128], identity=IDB[:])
                        return ins
                    S.op("pe", trss, reads=[bSS[j], bCONST], writes=[bPS[mbank[j]]])
                    S.op("act", lambda e, j=j, pb=pb: e.copy(
                        out=STt[:, :, j * 128:(j + 1) * 128],
                        in_=pb[:, 0:512].rearrange("p (c n) -> p c n", c=4)),
                         reads=[bPS[mbank[j]]], writes=[bST])

            def partConv(gi, t0, n, is_s):
                N = n * 128
                for c in range(4):
                    def cv(e, c=c):
                        if is_s:
                            for s_ in range(2):
                                for jt in range(KW):
                                    for i in range(4):
                                        p0 = 32 * i
                                        ins = e.matmul(PS[p0:p0 + 32, c, s_ * 64:(s_ + 1) * 64],
                                                       lhsT=D32[p0:p0 + 32, jt, c, :], rhs=AS[p0:p0 + 32, c, s_, jt:jt + 64],
                                                       start=(jt == 0), stop=(jt == KW - 1), tile_position=(p0, p0))
                            return ins
                        for jt in range(KW):
                            for i in range(4):
                                p0 = 32 * i
                                ins = e.matmul(PS[p0:p0 + 32, c, 0:N], lhsT=D32[p0:p0 + 32, jt, c, :],
                                               rhs=ABF[p0:p0 + 32, c, jt:jt + N],
                                               start=(jt == 0), stop=(jt == KW - 1), tile_position=(p0, p0))
                        return ins
                    S.op("pe", cv, reads=[bABUF, bPAR], writes=[bPS[c]])
                    S.op("act", lambda e, c=c: e.activation(out=ACC[:, c, 0:N], in_=PS[:, c, 0:N], func=AF.Identity,
                                                            bias=CB[:, c:c + 1], scale=1.0),
                         reads=[bPS[c], bPAR], writes=[bACC[c]])

            def partStats(gi, t0, n, is_s):
                N = n * 128
                for c in range(4):
                    qi = rr["sq"] % 2
                    rr["sq"] += 1
                    S.op("act", lambda e, c=c, qi=qi: e.activation(out=SQ[qi][:, 0:N], in_=ACC[:, c, 0:N], func=AF.Square),
                         reads=[bACC[c]], writes=[bSQ[qi]])
                    S.op("pe", lambda e, c=c: e.matmul(PS[:, 4, 0:N], lhsT=ONESC[:], rhs=ACC[:, c, 0:N],
                                                       start=(c == 0), stop=(c == 3)),
                         reads=[bACC[c], bCONST], writes=[bPS[4]])
                    S.op("pe", lambda e, c=c, qi=qi: e.matmul(PS[:, 5, 0:N], lhsT=ONESC[:], rhs=SQ[qi][:, 0:N],
                                                              start=(c == 0), stop=(c == 3)),
                         reads=[bSQ[qi], bCONST], writes=[bPS[5]])
                S.op("act", lambda e: e.copy(out=MEANB[:, 0:N], in_=PS[:, 4, 0:N]), reads=[bPS[4]], writes=[bGT[0]])
                S.op("dve", lambda e: e.tensor_tensor(out=RSTDB[:, 0:N], in0=MEANB[:, 0:N], in1=MEANB[:, 0:N],
                                                      op=ALU.mult),
                     reads=[bGT[0]], writes=[bGT[1]])
                S.op("dve", lambda e: e.tensor_tensor(out=RSTDB[:, 0:N], in0=PS[:, 5, 0:N], in1=RSTDB[:, 0:N],
                                                      op=ALU.subtract),
                     reads=[bPS[5], bGT[1]], writes=[bGT[1]])
                S.op("act", lambda e: e.activation(out=RSTDB[:, 0:N], in_=RSTDB[:, 0:N], func=AF.Sqrt,
                                                   bias=EPS1[:, 0:1], scale=1.0),
                     reads=[bGT[1], bCONST], writes=[bGT[1]])
                S.op("dve", lambda e: e.reciprocal(out=RSTDB[:, 0:N], in_=RSTDB[:, 0:N]), reads=[bGT[1]], writes=[bGT[1]])

            def partNorm(gi, t0, n, is_s, chunks=range(4)):
                N = n * 128
                for c in chunks:
                    S.op("dve", lambda e, c=c: e.tensor_tensor(out=ACC[:, c, 0:N], in0=ACC[:, c, 0:N],
                                                               in1=MEANB[:, 0:N], op=ALU.subtract),
                         reads=[bACC[c], bGT[0], bGT[1]], writes=[bACC[c]])
                    S.op("dve", lambda e, c=c: e.tensor_tensor(out=ACC[:, c, 0:N], in0=ACC[:, c, 0:N],
                                                               in1=RSTDB[:, 0:N], op=ALU.mult),
                         reads=[bACC[c], bGT[0], bGT[1]], writes=[bACC[c]])
                    S.op("act", lambda e, c=c: e.activation(out=CT[:, c, 0:N], in_=ACC[:, c, 0:N], func=AF.Silu,
                                                            scale=CLG[:, c:c + 1], bias=CLB[:, c:c + 1]),
                         reads=[bACC[c], bPAR], writes=[bCT])

            def partOut(gi, t0, n, is_s):
                for j in range(n):
                    t = t0 + j
                    mb = 4 + 2 * (rr["mx"] % 2)
                    rr["mx"] += 1

                    def mmo(e, j=j, mb=mb):
                        for half in range(2):
                            Wo = WoutA if half == 0 else WoutB
                            for k in range(8):
                                lhs = CT[:, k, j * 128:(j + 1) * 128] if k < 4 else STt[:, k - 4, j * 128:(j + 1) * 128]
                                ins = e.matmul(PS[:, mb + half, :], lhsT=lhs, rhs=Wo[:, k, :],
                                               start=(k == 0), stop=(k == 7))
                        return ins
                    S.op("pe", mmo, reads=[bRING[sA], bRING[sB], bCT, bST], writes=bb(mb) + bb(mb + 1))
                    msrc = PS[:, mb:mb + 2, :].rearrange("p a n -> p (a n)")
                    S.op("dve", lambda e, t=t, msrc=msrc: e.scalar_tensor_tensor(
                        out=X[:, t, :], in0=X[:, t, :], scalar=ALPHA, in1=msrc, op0=ALU.mult, op1=ALU.add),
                         reads=[bX[t]] + bb(mb) + bb(mb + 1), writes=[bX[t]])
                layer_norm_group([t0 + j for j in range(n)], EPS1)

            partA(0, *groups[0])
            for gi, g in enumerate(groups):
                partB(gi, *g)
                if gi + 1 < len(groups):
                    ensure_xt([groups[gi + 1][0] + j for j in range(groups[gi + 1][1])])
                partConv(gi, *g)
                partSGU(gi, *g)
                partStats(gi, *g)
                if gi + 1 < len(groups):
                    partA(gi + 1, *groups[gi + 1], between=lambda c, gi=gi, g=g: partNorm(gi, *g, chunks=[c]))
                else:
                    partNorm(gi, *g)
                partOut(gi, *g)
                if gi >= 1:
                    ensure_xt([groups[gi - 1][0] + j for j in range(groups[gi - 1][1])])
            load_next()
            load_next()

        for _ in range(NSLOT):
            load_next()
        prep_dma(0)
        for b in range(NB):
            for t in range(NT):
                S.dma("sp", X[:, t, :], xin[(b * NT + t) * 128:(b * NT + t + 1) * 128, :], writes=[bX[t]],
                      sem="xin%d" % (t // 3))
            for t in range(NT):
                i = rr["ln"] % 2
                rr["ln"] += 1
                S.op("act", lambda e, t=t, i=i: e.copy(out=XN[i][:], in_=X[:, t, :]), reads=[bX[t]], writes=[bXN[i]])
                transposes_to_XT(t, i, 0 if t % 2 == 0 else 2)
            for l in range(NL):
                prep_pe(l)
                if dbg_stage == "x0" and l == 0:
                    for t in range(NT):
                        S.dma("sp", dbg_d[(b * NT + t) * 128:(b * NT + t + 1) * 128, :], X[:, t, :], reads=[bX[t]], sem="sout")
                ffn(l, 1, l == NL - 1)
                if dbg_stage == "x1" and l == 0:
                    for t in range(NT):
                        S.dma("sp", dbg_d[(b * NT + t) * 128:(b * NT + t + 1) * 128, :], X[:, t, :], reads=[bX[t]], sem="sout")
                mixer(l, b)
                if dbg_stage == "x2" and l == 0:
                    for t in range(NT):
                        S.dma("sp", dbg_d[(b * NT + t) * 128:(b * NT + t + 1) * 128, :], X[:, t, :], reads=[bX[t]], sem="sout")
                if l + 1 < NL:
                    prep_dma(l + 1)
                elif b + 1 < NB:
                    prep_dma(0)
                ffn(l, 2, l == NL - 1)
            for t in range(NT):
                gt = b * NT + t
                if gt == 0:
                    continue
                S.dma("sp", y_d[(gt - 1) * 128:gt * 128, :], X[:, t, :], reads=[bX[t]], sem="yout")
        S.wait_all("sp", ["yout", "sout"])
        block = es.enter_context(nc.Block())
        S.emit(block)
    return nc


_WNAMES = ["w_ffn1_gate", "w_ffn1_up", "w_ffn1_down", "w_ffn2_gate", "w_ffn2_up", "w_ffn2_down", "w_in", "w_out",
           "ln1_g", "ln1_b", "ln2_g", "ln2_b", "ln3_g", "ln3_b", "conv_k", "conv_b", "conv_ln_g", "conv_ln_b",
           "sgu_ln_g", "sgu_ln_b", "w_sgu", "b_sgu"]


def make_in_maps(inputs, cores=range(NCORES)):
    xp = np.asarray(inputs["x_prompt"], dtype=np.float32)
    xs = np.asarray(inputs["x_sample"], dtype=np.float32)
    cc = np.asarray(inputs["cache_conv"], dtype=np.float32)
    W = {k: np.ascontiguousarray(np.asarray(inputs[k], dtype=np.float32)) for k in _WNAMES}
    maps = []
    for c in cores:
        seq, part = c // 4, c % 4
        s0 = part * OWN
        xin = np.zeros((NB * NT * 128, D), np.float32)
        if part > 0:
            xin[0:128] = xp[seq, s0 - 128:s0]
        xin[128:128 + OWN] = xp[seq, s0:s0 + OWN]
        xin[128 + OWN:] = xs[2 * c:2 * c + 2].reshape(128, D)
        m = {"xin": xin,
             "cmask": np.full((128, 1), 0.0 if part == 0 else 1.0, np.float32),
             "cache": np.ascontiguousarray(cc[:, 2 * c:2 * c + 2])}
        m.update(W)
        maps.append(m)
    return maps


_NC_CACHE = {}


def kernel(**inputs):
    if "nc" not in _NC_CACHE:
        _NC_CACHE["nc"] = build_program(DEPTH)
    nc = _NC_CACHE["nc"]
    maps = make_in_maps(inputs)
    res = run_bass_kernel_spmd(nc, maps, core_ids=list(range(NCORES)))
    R = res.results
    y_prompt = np.zeros((2, 8192, D), np.float32)
    y_sample = np.zeros((16, 64, D), np.float32)
    conv_p = np.zeros((DEPTH, 2, CTXN, DC), np.float32)
    conv_s = np.zeros((DEPTH, 16, CTXN, DC), np.float32)
    sgu_v = np.zeros((DEPTH, 16, 64, DC), np.float32)
    for c in range(NCORES):
        seq, part = c // 4, c % 4
        y = np.asarray(R[c]["y"])
        y_prompt[seq, part * OWN:(part + 1) * OWN] = y[0:OWN]
        y_sample[2 * c:2 * c + 2] = y[OWN:].reshape(2, 64, D)
        if part == 3:
            conv_p[:, seq] = np.asarray(R[c]["conv_p"])
        conv_s[:, 2 * c:2 * c + 2] = np.asarray(R[c]["conv_s"])
        sgu_v[:, 2 * c:2 * c + 2] = np.asarray(R[c]["sgu_v"]).reshape(DEPTH, 2, 64, DC)
    return (y_prompt, y_sample, conv_p, conv_s, sgu_v)
```

```python
import numpy as np
import concourse.bass as bass
import concourse.mybir as mybir
from concourse.bass_utils import run_bass_kernel_spmd
from contextlib import ExitStack

F32 = mybir.dt.float32
BF16 = mybir.dt.bfloat16
AF = mybir.ActivationFunctionType
ALU = mybir.AluOpType

D = 1024
DFF = 2816
NF = 22
DEPTH = 4
DC = 512
NH = 8
HD = 64
KW = 31
CTXN = 30
NB = 2
NT = 9
NCORES = 8
OWN = 2048
ALPHA = float((2 * DEPTH) ** 0.25)
EPS = 1e-5
FG = 4
FGROUPS = [(0, 2), (2, 4), (6, 4), (10, 4), (14, 4), (18, 4)]
NSLOT = 3
SLOTW = 12288


class Buf:
    __slots__ = ("name", "w", "r")

    def __init__(self, name):
        self.name = name
        self.w = None
        self.r = []


class Sched:
    COMPUTE = ("pe", "act", "dve", "pool")

    def __init__(self, nc, es):
        self.nc = nc
        self.es = es
        self.eng = {"pe": nc.tensor, "act": nc.scalar, "dve": nc.vector, "pool": nc.gpsimd, "sp": nc.sync}
        self.sems = {}
        self.cnt = {}
        self.isdma = {}
        for e in self.COMPUTE:
            self.sems[e] = es.enter_context(nc.semaphore("tk_" + e))
            self.cnt[e] = 0
            self.isdma[e] = False
        self.prog = {e: [] for e in self.eng}
        self.waited = {e: {} for e in self.eng}

    def dma_sem(self, name):
        if name not in self.sems:
            self.sems[name] = self.es.enter_context(self.nc.semaphore("dq_" + name))
            self.cnt[name] = 0
            self.isdma[name] = True
        return name

    def _deps(self, eng, reads, writes):
        deps = {}

        def add(ev):
            if ev is None:
                return
            k, v = ev
            if deps.get(k, 0) < v:
                deps[k] = v
        for b in reads:
            add(b.w)
        for b in writes:
            add(b.w)
            for r in b.r:
                add(r)
        waits = []
        for k, v in deps.items():
            if k == "pe" and eng == "pe":
                continue
            if self.isdma[k]:
                v = self.cnt[k]
            if self.waited[eng].get(k, 0) >= v:
                continue
            self.waited[eng][k] = v
            waits.append((k, v))
        return waits

    def _commit(self, ev, reads, writes):
        for b in reads:
            b.r.append(ev)
            if len(b.r) > 64:
                m = {}
                for k, v in b.r:
                    if m.get(k, 0) < v:
                        m[k] = v
                b.r = list(m.items())
        for b in writes:
            b.w = ev
            b.r = []

    def op(self, eng, fn, reads=(), writes=()):
        waits = self._deps(eng, reads, writes)
        self.cnt[eng] += 1
        ev = (eng, self.cnt[eng])
        self._commit(ev, reads, writes)
        self.prog[eng].append((waits, fn, (eng, 1)))
        return ev

    def dma(self, q, out, in_, reads=(), writes=(), sem="misc", **kw):
        self.dma_sem(sem)
        waits = self._deps(q, reads, writes)
        self.cnt[sem] += 16
        ev = (sem, self.cnt[sem])
        self._commit(ev, reads, writes)
        self.prog[q].append((waits, lambda e: e.dma_start(out=out, in_=in_, **kw), (sem, 16)))
        return ev

    def wait_all(self, eng, semnames):
        waits = [(k, self.cnt[k]) for k in semnames if self.cnt.get(k, 0) > 0]
        self.prog[eng].append((waits, None, None))

    def emit(self, block):
        def run(engname):
            def body(e):
                for waits, fn, inc in self.prog[engname]:
                    for k, v in waits:
                        e.wait_ge(self.sems[k], v)
                    if fn is not None:
                        ins = fn(e)
                        ins.then_inc(self.sems[inc[0]], inc[1])
            return body
        block.tensor(run("pe"))
        block.scalar(run("act"))
        block.vector(run("dve"))
        block.gpsimd(run("pool"))
        block.sync(run("sp"))


def build_program(NL=DEPTH, dbg_stage=None):
    nc = bass.Bass("TRN2", target_bir_lowering=False)
    TOK = NB * NT * 128

    def din(name, shape):
        return nc.dram_tensor(name, list(shape), F32, kind="ExternalInput").ap()

    def dout(name, shape):
        return nc.dram_tensor(name, list(shape), F32, kind="ExternalOutput").ap()

    xin = din("xin", [TOK, D])
    cmask_d = din("cmask", [128, 1])
    cache_d = din("cache", [DEPTH, 2, CTXN, DC])
    wd = {}
    for nm, shp in [("w_ffn1_gate", [DEPTH, D, DFF]), ("w_ffn1_up", [DEPTH, D, DFF]), ("w_ffn1_down", [DEPTH, DFF, D]),
                    ("w_ffn2_gate", [DEPTH, D, DFF]), ("w_ffn2_up", [DEPTH, D, DFF]), ("w_ffn2_down", [DEPTH, DFF, D]),
                    ("w_in", [DEPTH, D, 2048]), ("w_out", [DEPTH, D, D]),
                    ("ln1_g", [DEPTH, D]), ("ln1_b", [DEPTH, D]), ("ln2_g", [DEPTH, D]), ("ln2_b", [DEPTH, D]),
                    ("ln3_g", [DEPTH, D]), ("ln3_b", [DEPTH, D]),
                    ("conv_k", [DEPTH, KW, DC]), ("conv_b", [DEPTH, DC]), ("conv_ln_g", [DEPTH, DC]),
                    ("conv_ln_b", [DEPTH, DC]), ("sgu_ln_g", [DEPTH, DC]), ("sgu_ln_b", [DEPTH, DC]),
                    ("w_sgu", [DEPTH, NH, 128, 128]), ("b_sgu", [DEPTH, NH, 128])]:
        wd[nm] = din(nm, shp)
    y_d = dout("y", [TOK - 128, D])
    convp_d = dout("conv_p", [DEPTH, CTXN, DC])
    convs_d = dout("conv_s", [DEPTH, 2, CTXN, DC])
    sguv_d = dout("sgu_v", [DEPTH, 128, DC])
    dbg_d = dout("dbg", [TOK, D]) if dbg_stage else None

    es = ExitStack()
    with es:
        es.enter_context(nc.allow_non_contiguous_dma(reason="small strided parameter loads"))
        S = Sched(nc, es)

        def sb(name, shape, dt=F32):
            return es.enter_context(nc.sbuf_tensor(name, list(shape), dt))

        X = sb("X", [128, NT, D])
        XT = sb("XT", [128, 8, NT * 128], BF16)
        RING = [sb("ring%d" % i, [128, SLOTW], BF16) for i in range(NSLOT)]
        GT = [sb("gt%d" % i, [128, FG, 384], BF16) for i in range(2)]
        SIL = [sb("sil%d" % i, [128, 384]) for i in range(2)]
        XN = [sb("xn%d" % i, [128, D], BF16) for i in range(2)]
        LNG = sb("lng", [128, 2, D])
        LNS = [dict(st=sb("lnst%d" % i, [128, 3, 2, 6]), mv=sb("lnmv%d" % i, [128, 3, 2]),
                    rstd=sb("lnrs%d" % i, [128, 3]), nmr=sb("lnnm%d" % i, [128, 3])) for i in range(2)]
        ABF = sb("abf", [128, 4, CTXN + 384], BF16)
        A30 = sb("a30", [128, 4, 2, CTXN])
        D32 = sb("d32", [128, KW, 4, 32], BF16)
        M32 = sb("m32", [128, 32])
        ACC = sb("acc", [128, 4, 384])
        SQ0 = sb("sq0", [128, 384])
        SQ = [SQ0, SQ0]
        CT = sb("ct", [128, 4, 384], BF16)
        STt = sb("stt", [128, 4, 384], BF16)
        U = [sb("u%d" % i, [128, DC]) for i in range(3)]
        ZV = [sb("zv%d" % i, [128, DC]) for i in range(3)]
        VB = [sb("vb%d" % i, [128, DC], BF16) for i in range(3)]
        SS = VB
        MEANB = GT[0][:].rearrange("p f n -> p (f n)").bitcast(F32)[:, 0:384]
        RSTDB = GT[1][:].rearrange("p f n -> p (f n)").bitcast(F32)[:, 0:384]
        SGB = sb("sgb", [128, 2, DC])
        WT = sb("wt", [128, NH, 128], BF16)
        WTS = sb("wts", [128, NH, 128], BF16)
        WNAT = sb("wnat", [128, NH, 128], BF16)
        WNATS = sb("wnats", [128, NH, 128], BF16)
        BIAS = sb("bias", [128, NH])
        BIASS = sb("biass", [128, NH])
        MASKB = sb("maskb", [128, 128], BF16)
        SCR1 = sb("scr1", [124, DC])
        KNAT = SCR1[0:KW, :]
        CNAT = SCR1[64:64 + 2 * CTXN, :]
        KCOL = sb("kcol", [128, 4, 32])
        CB = sb("cb", [128, 4])
        CLG = sb("clg", [128, 4])
        CLB = sb("clb", [128, 4])
        CTX = sb("ctx", [128, DEPTH, 4, CTXN], BF16)
        CTXS = sb("ctxs", [128, 4, 2, CTXN], BF16)
        CMASK = sb("cmaskt", [128, 1])
        EPS1 = sb("eps1", [128, 1])
        EPS4 = sb("eps4", [128, 1])
        IDF = sb("idf", [128, 128])
        IDB = sb("idb", [128, 128], BF16)
        ONESC = sb("onesc", [128, 128])
        STG = U[2][0:CTXN, :]
        PS = es.enter_context(nc.psum_tensor("psall", [128, 8, 512], F32))

        def psbf(bank):
            return PS[:, bank, :].bitcast(BF16)

        bX = [Buf("X%d" % t) for t in range(NT)]
        bXT = [Buf("XT%d" % t) for t in range(NT)]
        bRING = [Buf("ring%d" % i) for i in range(NSLOT)]
        bGT = [Buf("gt%d" % i) for i in range(2)]
        bSIL = [Buf("sil%d" % i) for i in range(2)]
        bXN = [Buf("xn%d" % i) for i in range(2)]
        bLNG = Buf("lng")
        bLNS = [Buf("lns%d" % i) for i in range(2)]
        bLNSr = [Buf("lnsr%d" % i) for i in range(2)]
        bABUF = Buf("abuf")
        bACC = [Buf("acc%d" % c) for c in range(4)]
        bSQ0 = Buf("sq0")
        bSQ = [bSQ0, bSQ0]
        bCT = Buf("ct")
        bST = Buf("stt")
        bU = [Buf("u%d" % i) for i in range(3)]
        bZV = [Buf("zv%d" % i) for i in range(3)]
        bVB = [Buf("vb%d" % i) for i in range(3)]
        bSS = bVB
        bPAR = Buf("layer_params")
        bWNAT = Buf("wnat")
        bKNAT = Buf("knat")
        bCNAT = Buf("cnat")
        bCTX = [Buf("ctx%d" % l) for l in range(DEPTH)]
        bCONST = Buf("const")
        bSTG = bU[2]
        bA30 = Buf("a30")
        bPS = [Buf("ps%d" % i) for i in range(8)]
        bPS7h = [bPS[7]]

        def bb(bank):
            return [bPS[bank]]

        S.op("pool", lambda e: e.memset(IDF[:], 0.0), writes=[bCONST])
        S.op("pool", lambda e: e.affine_select(out=IDF[:], in_=IDF[:], pattern=[[-1, 128]], compare_op=ALU.not_equal,
                                               fill=1.0, base=0, channel_multiplier=1), reads=[bCONST], writes=[bCONST])
        S.op("pool", lambda e: e.memset(ONESC[:], 1.0), writes=[bCONST])
        S.op("pool", lambda e: e.affine_select(out=ONESC[:], in_=ONESC[:], pattern=[[1, 128]], compare_op=ALU.is_ge,
                                               fill=0.0, base=0, channel_multiplier=-1), reads=[bCONST], writes=[bCONST])

        def c_misc(e):
            e.memset(EPS1[:], EPS)
            e.memset(EPS4[:], 4.0 * EPS)
            e.memset(CTX[:], 0.0)
            e.memset(KCOL[:], 0.0)
            return e.memset(WNATS[:], 0.0)
        S.op("pool", c_misc, writes=[bCONST, bWNAT, bPAR] + bCTX)
        S.op("act", lambda e: e.copy(out=IDB[:], in_=IDF[:]), reads=[bCONST], writes=[bCONST])
        S.op("act", lambda e: e.copy(out=MASKB[:], in_=ONESC[:]), reads=[bCONST], writes=[bCONST])
        S.op("dve", lambda e: e.memset(ONESC[:], 1.0 / DC), reads=[bCONST], writes=[bCONST])
        S.op("dve", lambda e: e.tensor_tensor(out=M32[:], in0=IDF[:, 0:32], in1=IDF[:, 32:64], op=ALU.add),
             reads=[bCONST], writes=[bCONST])
        S.op("dve", lambda e: e.tensor_tensor(out=M32[:], in0=M32[:], in1=IDF[:, 64:96], op=ALU.add),
             reads=[bCONST], writes=[bCONST])
        S.op("dve", lambda e: e.tensor_tensor(out=M32[:], in0=M32[:], in1=IDF[:, 96:128], op=ALU.add),
             reads=[bCONST], writes=[bCONST])
        S.dma("sp", CMASK[:], cmask_d, writes=[bCONST], sem="cst")

        units = []
        for b in range(NB):
            for l in range(NL):
                for g in range(len(FGROUPS)):
                    units.append(("F", l, 1, g))
                units.append(("MA", l))
                units.append(("MB", l))
                for g in range(len(FGROUPS)):
                    units.append(("F", l, 2, g))
        ring_state = {"next": 0}

        def load_next():
            u = ring_state["next"]
            if u >= len(units):
                return
            ring_state["next"] = u + 1
            slot = u % NSLOT
            R = RING[slot]
            sem = "ring%d" % slot
            un = units[u]
            if un[0] == "F":
                _, l, which, g = un
                f0, n = FGROUPS[g]
                wg = wd["w_ffn%d_gate" % which]
                wu = wd["w_ffn%d_up" % which]
                wdn = wd["w_ffn%d_down" % which]
                S.dma("pool", R[:, 0:8 * n * 128].rearrange("p (k n) -> p k n", k=8),
                      wg[l, :, f0 * 128:(f0 + n) * 128].rearrange("(k p) n -> p k n", p=128),
                      writes=[bRING[slot]], sem=sem)
                S.dma("pool", R[:, 4096:4096 + 8 * n * 128].rearrange("p (k n) -> p k n", k=8),
                      wu[l, :, f0 * 128:(f0 + n) * 128].rearrange("(k p) n -> p k n", p=128),
                      writes=[bRING[slot]], sem=sem)
                S.dma("pool", R[:, 8192:8192 + n * D].rearrange("p (f n) -> p f n", f=n),
                      wdn[l, f0 * 128:(f0 + n) * 128, :].rearrange("(f p) n -> p f n", p=128),
                      writes=[bRING[slot]], sem=sem)
            else:
                l = un[1]
                c0 = 0 if un[0] == "MA" else 1024
                o0 = 0 if un[0] == "MA" else 512
                S.dma("pool", R[:, 0:8192].rearrange("p (k n) -> p k n", k=8),
                      wd["w_in"][l, :, c0:c0 + 1024].rearrange("(k p) n -> p k n", p=128),
                      writes=[bRING[slot]], sem=sem)
                S.dma("pool", R[:, 8192:12288].rearrange("p (k n) -> p k n", k=8),
                      wd["w_out"][l, :, o0:o0 + 512].rearrange("(k p) n -> p k n", p=128),
                      writes=[bRING[slot]], sem=sem)

        cur_unit = {"i": 0}

        def take_unit():
            u = cur_unit["i"]
            cur_unit["i"] = u + 1
            return u % NSLOT

        rr = {"ln": 0, "trb": 0, "gu": 0, "y": 0, "gtb": 0, "mx": 0, "bt": 0, "sq": 0}
        deferred = {}

        def ensure_xt(tiles):
            for t in tiles:
                fn = deferred.pop(t, None)
                if fn is not None:
                    fn()

        def flush_deferred():
            ensure_xt(sorted(deferred.keys()))

        def transposes_to_XT(t, xn_i, bank):
            pb = psbf(bank)

            def tr(e):
                for k in range(8):
                    ins = e.transpose(out=pb[:, k * 128:(k + 1) * 128], in_=XN[xn_i][:, k * 128:(k + 1) * 128],
                                      identity=IDB[:])
                return ins
            S.op("pe", tr, reads=[bXN[xn_i], bCONST], writes=[bPS[bank]])
            S.op("act", lambda e: e.copy(out=XT[:, :, t * 128:(t + 1) * 128],
                                         in_=pb.rearrange("p (k n) -> p k n", k=8)),
                 reads=[bPS[bank]], writes=[bXT[t]])

        def layer_norm_group(tiles, eps_tile, need_xt=True):
            i = rr["ln"] % 2
            rr["ln"] += 1
            L = LNS[i]
            n = len(tiles)
            for k, t in enumerate(tiles):
                def stats(e, k=k, t=t):
                    e.bn_stats(out=L["st"][:, k, 0, :], in_=X[:, t, 0:512])
                    return e.bn_stats(out=L["st"][:, k, 1, :], in_=X[:, t, 512:1024])
                S.op("dve", stats, reads=[bX[t]], writes=[bLNS[i], bLNSr[i]])
            for k, t in enumerate(tiles):
                S.op("dve", lambda e, k=k: e.bn_aggr(out=L["mv"][:, k, :], in_=L["st"][:, k, :, :]),
                     reads=[bLNS[i]], writes=[bLNS[i]])
            S.op("act", lambda e: e.activation(out=L["rstd"][:, 0:n], in_=L["mv"][:, 0:n, 1], func=AF.Sqrt,
                                               bias=eps_tile[:, 0:1], scale=1.0),
                 reads=[bLNS[i], bCONST], writes=[bLNSr[i]])
            for k, t in enumerate(tiles):
                S.op("dve", lambda e, k=k, t=t: e.scalar_tensor_tensor(
                    out=X[:, t, :], in0=X[:, t, :], scalar=L["mv"][:, k, 0:1], in1=LNG[:, 0, :],
                    op0=ALU.subtract, op1=ALU.mult),
                     reads=[bX[t], bLNS[i], bLNG], writes=[bX[t]])
            S.op("dve", lambda e: e.reciprocal(out=L["rstd"][:, 0:n], in_=L["rstd"][:, 0:n]),
                 reads=[bLNSr[i]], writes=[bLNSr[i]])
            for k, t in enumerate(tiles):
                S.op("dve", lambda e, k=k, t=t: e.scalar_tensor_tensor(
                    out=X[:, t, :], in0=X[:, t, :], scalar=L["rstd"][:, k:k + 1], in1=LNG[:, 1, :],
                    op0=ALU.mult, op1=ALU.add),
                     reads=[bX[t], bLNSr[i], bLNG], writes=[bX[t]])
            if need_xt:
                for t in tiles:
                    def later(t=t):
                        xi = rr["trb"] % 2
                        rr["trb"] += 1
                        S.op("act", lambda e: e.copy(out=XN[xi][:], in_=X[:, t, :]), reads=[bX[t]], writes=[bXN[xi]])
                        transposes_to_XT(t, xi, 0 if xi == 0 else 2)
                    deferred[t] = later

        def load_lng(gname, bname, l):
            S.dma("sp", LNG[:, 0, :], wd[gname][l:l + 1, :].partition_broadcast(128), writes=[bLNG], sem="lng")
            S.dma("sp", LNG[:, 1, :], wd[bname][l:l + 1, :].partition_broadcast(128), writes=[bLNG], sem="lng")

        def ffn(l, which, last_layer):
            load_lng("ln1_g" if which == 1 else "ln3_g", "ln1_b" if which == 1 else "ln3_b", l)
            slots = {}
            sched = [(g, tg) for g in range(len(FGROUPS)) for tg in range(NT // 3)]

            def slot_of(g):
                if g not in slots:
                    slots[g] = take_unit()
                return slots[g]

            gt_of = {}

            def P1(idx):
                g, tg = sched[idx]
                f0, n = FGROUPS[g]
                slot = slot_of(g)
                R = RING[slot]
                Wg = R[:, 0:8 * n * 128].rearrange("p (k n) -> p k n", k=8)
                Wu = R[:, 4096:4096 + 8 * n * 128].rearrange("p (k n) -> p k n", k=8)
                gi = rr["gtb"] % 2
                rr["gtb"] += 1
                gt_of[idx] = gi
                c0 = tg * 384
                ensure_xt([tg * 3 + j for j in range(3)])
                xbufs = [bXT[tg * 3 + j] for j in range(3)]
                for fi in range(n):
                    bp = rr["gu"] % 2
                    rr["gu"] += 1
                    gb, ub = 2 * bp, 2 * bp + 1

                    def mm(e, W=None, bank=0, fi=fi):
                        for k in range(8):
                            ins = e.matmul(PS[:, bank, 0:384], lhsT=W[:, k, fi * 128:(fi + 1) * 128],
                                           rhs=XT[:, k, c0:c0 + 384], start=(k == 0), stop=(k == 7))
                        return ins
                    S.op("pe", lambda e, mm=mm, gb=gb: mm(e, Wg, gb), reads=[bRING[slot]] + xbufs, writes=[bPS[gb]])
                    S.op("pe", lambda e, mm=mm, ub=ub: mm(e, Wu, ub), reads=[bRING[slot]] + xbufs, writes=[bPS[ub]])
                    S.op("act", lambda e, gb=gb, bp=bp: e.activation(out=SIL[bp][:], in_=PS[:, gb, 0:384], func=AF.Silu),
                         reads=[bPS[gb]], writes=[bSIL[bp]])
                    S.op("dve", lambda e, ub=ub, bp=bp, fi=fi, gi=gi: e.tensor_tensor(
                        out=GT[gi][:, fi, :], in0=SIL[bp][:], in1=PS[:, ub, 0:384], op=ALU.mult),
                         reads=[bSIL[bp], bPS[ub]], writes=[bGT[gi]])

            def P2(idx):
                g, tg = sched[idx]
                f0, n = FGROUPS[g]
                slot = slot_of(g)
                R = RING[slot]
                Wd_ = R[:, 8192:8192 + n * D].rearrange("p (f n) -> p f n", f=n)
                gi = gt_of[idx]
                for j in range(3):
                    t = tg * 3 + j
                    yb = 4 + 2 * (rr["y"] % 2)
                    rr["y"] += 1

                    def mm(e, j=j, yb=yb):
                        for half in range(2):
                            for fi in range(n):
                                ins = e.matmul(PS[:, yb + half, :], lhsT=GT[gi][:, fi, j * 128:(j + 1) * 128],
                                               rhs=Wd_[:, fi, half * 512:(half + 1) * 512],
                                               start=(fi == 0), stop=(fi == n - 1))
                        return ins
                    S.op("pe", mm, reads=[bRING[slot], bGT[gi]], writes=bb(yb) + bb(yb + 1))
                    ysrc = PS[:, yb:yb + 2, :].rearrange("p a n -> p (a n)")
                    if g == 0:
                        S.op("dve", lambda e, t=t, ysrc=ysrc: e.scalar_tensor_tensor(
                            out=X[:, t, :], in0=X[:, t, :], scalar=2.0 * ALPHA, in1=ysrc, op0=ALU.mult, op1=ALU.add),
                             reads=[bX[t]] + bb(yb) + bb(yb + 1), writes=[bX[t]])
                    else:
                        S.op("dve", lambda e, t=t, ysrc=ysrc: e.tensor_tensor(
                            out=X[:, t, :], in0=X[:, t, :], in1=ysrc, op=ALU.add),
                             reads=[bX[t]] + bb(yb) + bb(yb + 1), writes=[bX[t]])
                if g == len(FGROUPS) - 1:
                    if tg >= 1:
                        ensure_xt([(tg - 1) * 3 + j for j in range(3)])
                    layer_norm_group([tg * 3 + j for j in range(3)], EPS4, need_xt=not (last_layer and which == 2))

            P1(0)
            for idx in range(len(sched)):
                if idx + 1 < len(sched):
                    P1(idx + 1)
                P2(idx)
                g, tg = sched[idx]
                if g == len(FGROUPS) - 1 and idx + 1 < len(sched):
                    pass
                if tg == NT // 3 - 1:
                    load_next()
            if last_layer and which == 2:
                flush_deferred()

        def prep_dma(l):
            S.dma("sp", KNAT, wd["conv_k"][l], writes=[bKNAT], sem="par")
            S.dma("sp", CNAT, cache_d[l].rearrange("b r c -> (b r) c"), writes=[bCNAT], sem="par")
            S.dma("sp", CB[:], wd["conv_b"][l].rearrange("(c p) -> p c", p=128), writes=[bPAR], sem="par")
            S.dma("sp", CLG[:], wd["conv_ln_g"][l].rearrange("(c p) -> p c", p=128), writes=[bPAR], sem="par")
            S.dma("sp", CLB[:], wd["conv_ln_b"][l].rearrange("(c p) -> p c", p=128), writes=[bPAR], sem="par")
            S.dma("sp", SGB[:, 0, :], wd["sgu_ln_g"][l:l + 1, :].partition_broadcast(128), writes=[bPAR], sem="par")
            S.dma("sp", SGB[:, 1, :], wd["sgu_ln_b"][l:l + 1, :].partition_broadcast(128), writes=[bPAR], sem="par")
            S.dma("sp", BIAS[:], wd["b_sgu"][l].rearrange("h t -> t h"), writes=[bPAR], sem="par")
            S.dma("sp", BIASS[0:64, :], wd["b_sgu"][l, :, 0:64].rearrange("h t -> t h"), writes=[bPAR], sem="par")
            S.dma("sp", BIASS[64:128, :], wd["b_sgu"][l, :, 0:64].rearrange("h t -> t h"), writes=[bPAR], sem="par")
            S.dma("pool", WNAT[:], wd["w_sgu"][l].rearrange("h t s -> t h s"), writes=[bWNAT], sem="wn")
            S.dma("pool", WNATS[0:64, :, 0:64], wd["w_sgu"][l, :, 0:64, 0:64].rearrange("h t s -> t h s"),
                  writes=[bWNAT], sem="wn")
            S.dma("pool", WNATS[64:128, :, 64:128], wd["w_sgu"][l, :, 0:64, 0:64].rearrange("h t s -> t h s"),
                  writes=[bWNAT], sem="wn")

        def prep_pe(l):
            pk = PS[:, 7, 0:128].rearrange("p (c j) -> p c j", c=4)

            def trk(e):
                for c in range(4):
                    ins = e.transpose(out=pk[:, c, 0:KW], in_=SCR1[0:KW, c * 128:(c + 1) * 128],
                                      identity=IDF[0:KW, 0:KW])
                return ins
            S.op("pe", trk, reads=[bKNAT, bCONST], writes=bPS7h)
            S.op("act", lambda e: e.copy(out=KCOL[:, :, 0:KW], in_=pk[:, :, 0:KW]), reads=bPS7h, writes=[bPAR])
            for c in range(4):
                S.op("dve", lambda e, c=c: e.scalar_tensor_tensor(
                    out=D32[:, :, c, :], in0=M32[:].unsqueeze(1).to_broadcast([128, KW, 32]), scalar=0.5,
                    in1=KCOL[:, c, 0:KW].unsqueeze(2).to_broadcast([128, KW, 32]), op0=ALU.mult, op1=ALU.mult),
                     reads=[bPAR, bCONST], writes=[bPAR])
            pc = PS[:, 7, 0:256].rearrange("p (c j) -> p c j", c=4)

            def trc(e):
                for c in range(4):
                    ins = e.transpose(out=pc[:, c, 0:2 * CTXN], in_=SCR1[64:64 + 2 * CTXN, c * 128:(c + 1) * 128],
                                      identity=IDF[64:64 + 2 * CTXN, 64:64 + 2 * CTXN])
                return ins
            S.op("pe", trc, reads=[bCNAT, bCONST], writes=bPS7h)
            S.op("act", lambda e: e.activation(out=CTXS[:].rearrange("p c s j -> p c (s j)"), in_=pc[:, :, 0:2 * CTXN],
                                               func=AF.Identity, scale=2.0),
                 reads=bPS7h, writes=[bPAR])
            for (src, dst) in ((WNAT, WT), (WNATS, WTS)):
                pb = psbf(7)

                def trw(e, src=src):
                    for h in range(NH):
                        ins = e.transpose(out=pb[:, h * 128:(h + 1) * 128], in_=src[:, h, :], identity=IDB[:])
                    return ins
                S.op("pe", trw, reads=[bWNAT, bCONST], writes=bPS7h)
                S.op("dve", lambda e, dst=dst, pb=pb: e.tensor_tensor(
                    out=dst[:], in0=pb.rearrange("p (h t) -> p h t", h=NH),
                    in1=MASKB[:].unsqueeze(1).to_broadcast([128, NH, 128]), op=ALU.mult),
                     reads=bPS7h + [bCONST], writes=[bPAR])

        def mixer(l, b):
            load_lng("ln2_g", "ln2_b", l)
            sA = take_unit()
            sB = take_unit()
            WinA = RING[sA][:, 0:8192].rearrange("p (k n) -> p k n", k=8)
            WoutA = RING[sA][:, 8192:12288].rearrange("p (k n) -> p k n", k=8)
            WinB = RING[sB][:, 0:8192].rearrange("p (k n) -> p k n", k=8)
            WoutB = RING[sB][:, 8192:12288].rearrange("p (k n) -> p k n", k=8)
            if b == 0:
                groups = [(0, 3, False), (3, 3, False), (6, 3, False)]
            else:
                groups = [(0, 3, False), (3, 3, False), (6, 2, False), (8, 1, True)]
            AS = ABF[:, :, 0:188].rearrange("p c (s j) -> p c s j", s=2)

            def partA(gi, t0, n, is_s, between=None):
                N = n * 128
                c0 = t0 * 128
                ensure_xt([t0 + j for j in range(n)])
                xbufs = [bXT[t0 + j] for j in range(n)]
                want_state = is_s or (b == NB - 1 and gi == len(groups) - 2)
                if is_s:
                    S.op("act", lambda e: e.copy(out=AS[:, :, :, 0:CTXN], in_=CTXS[:]), reads=[bPAR], writes=[bABUF])
                else:
                    S.op("act", lambda e: e.copy(out=ABF[:, :, 0:CTXN], in_=CTX[:, l, :, :]),
                         reads=[bCTX[l]], writes=[bABUF])
                for c in range(4):
                    vb, gb = (2 * c) % 4, (2 * c + 1) % 4

                    def mm(e, col0, bank):
                        for k in range(8):
                            ins = e.matmul(PS[:, bank, 0:N], lhsT=WinA[:, k, col0:col0 + 128],
                                           rhs=XT[:, k, c0:c0 + N], start=(k == 0), stop=(k == 7))
                        return ins
                    S.op("pe", lambda e, mm=mm, c=c, vb=vb: mm(e, c * 128, vb), reads=[bRING[sA]] + xbufs, writes=[bPS[vb]])
                    S.op("pe", lambda e, mm=mm, c=c, gb=gb: mm(e, 512 + c * 128, gb), reads=[bRING[sA]] + xbufs,
                         writes=[bPS[gb]])
                    si = c % 2
                    S.op("act", lambda e, gb=gb, si=si: e.activation(out=SIL[si][:, 0:N], in_=PS[:, gb, 0:N],
                                                                     func=AF.Tanh, scale=0.5),
                         reads=[bPS[gb]], writes=[bSIL[si]])
                    if is_s:
                        S.op("dve", lambda e, c=c, vb=vb, si=si: e.scalar_tensor_tensor(
                            out=AS[:, c, :, CTXN:CTXN + 64],
                            in0=SIL[si][:, 0:128].rearrange("p (s j) -> p s j", s=2), scalar=1.0,
                            in1=PS[:, vb, 0:128].rearrange("p (s j) -> p s j", s=2), op0=ALU.add, op1=ALU.mult),
                             reads=[bPS[vb], bSIL[si]], writes=[bABUF])
                        S.op("dve", lambda e, c=c, vb=vb, si=si: e.scalar_tensor_tensor(
                            out=A30[:, c, :, :],
                            in0=SIL[si][:, 0:128].rearrange("p (s j) -> p s j", s=2)[:, :, 34:64], scalar=1.0,
                            in1=PS[:, vb, 0:128].rearrange("p (s j) -> p s j", s=2)[:, :, 34:64],
                            op0=ALU.add, op1=ALU.mult),
                             reads=[bPS[vb], bSIL[si]], writes=[bA30])
                    else:
                        first_halo = (b == 0 and gi == 0)

                        S.op("dve", lambda e, c=c, vb=vb, si=si: e.scalar_tensor_tensor(
                            out=ABF[:, c, CTXN:CTXN + N], in0=SIL[si][:, 0:N], scalar=1.0, in1=PS[:, vb, 0:N],
                            op0=ALU.add, op1=ALU.mult),
                             reads=[bPS[vb], bSIL[si]], writes=[bABUF])
                        if first_halo:
                            S.op("dve", lambda e, c=c: e.tensor_scalar(
                                out=ABF[:, c, CTXN:CTXN + 128], in0=ABF[:, c, CTXN:CTXN + 128], scalar1=CMASK[:, 0:1],
                                scalar2=None, op0=ALU.mult),
                                 reads=[bABUF, bCONST], writes=[bABUF])
                        if want_state:
                            S.op("dve", lambda e, c=c, vb=vb, si=si: e.scalar_tensor_tensor(
                                out=A30[:, c, 0, :], in0=SIL[si][:, N - CTXN:N], scalar=1.0, in1=PS[:, vb, N - CTXN:N],
                                op0=ALU.add, op1=ALU.mult),
                                 reads=[bPS[vb], bSIL[si]], writes=[bA30])
                    if between is not None:
                        between(c)
                if not is_s:
                    S.op("act", lambda e: e.copy(out=CTX[:, l, :, :], in_=ABF[:, :, N:N + CTXN]),
                         reads=[bABUF], writes=[bCTX[l]])
                if want_state:
                    nseq = 2 if is_s else 1
                    for s_ in range(nseq):
                        def trs(e, s_=s_):
                            for c in range(4):
                                ins = e.transpose(out=PS[0:CTXN, 7, c * 128:(c + 1) * 128], in_=A30[:, c, s_, :],
                                                  identity=IDF[:])
                            return ins
                        S.op("pe", trs, reads=[bA30, bCONST], writes=bPS7h)
                        S.op("act", lambda e: e.activation(out=STG, in_=PS[0:CTXN, 7, :], func=AF.Identity, scale=0.5),
                             reads=bPS7h, writes=[bSTG])
                        dst = convs_d[l, s_] if is_s else convp_d[l]
                        S.dma("sp", dst, STG, reads=[bSTG], sem="sout")

            def partB(gi, t0, n, is_s):
                li = rr["ln"] % 2
                rr["ln"] += 1
                L = LNS[li]
                for j in range(n):
                    t = t0 + j
                    hb = 4 + 2 * (j % 2)

                    def mmb(e, t=t, hb=hb):
                        for half in range(2):
                            for k in range(8):
                                ins = e.matmul(PS[:, hb + half, :], lhsT=XT[:, k, t * 128:(t + 1) * 128],
                                               rhs=WinB[:, k, half * 512:(half + 1) * 512],
                                               start=(k == 0), stop=(k == 7))
                        return ins
                    wr = bb(hb) + bb(hb + 1)
                    S.op("pe", mmb, reads=[bRING[sB], bXT[t]], writes=wr)
                    S.op("act", lambda e, j=j, hb=hb: e.activation(out=U[j][:], in_=PS[:, hb, :], func=AF.Gelu),
                         reads=[bPS[hb]], writes=[bU[j]])
                    S.op("act", lambda e, j=j, hb=hb: e.activation(out=ZV[j][:], in_=PS[:, hb + 1, :], func=AF.Gelu),
                         reads=bb(hb + 1), writes=[bZV[j]])
                for j in range(n):
                    S.op("dve", lambda e, j=j: e.bn_stats(out=L["st"][:, j, 0, :], in_=ZV[j][:]),
                         reads=[bZV[j]], writes=[bLNS[li], bLNSr[li]])
                for j in range(n):
                    S.op("dve", lambda e, j=j: e.bn_aggr(out=L["mv"][:, j, :], in_=L["st"][:, j, 0:1, :]),
                         reads=[bLNS[li]], writes=[bLNS[li]])
                S.op("act", lambda e: e.activation(out=L["rstd"][:, 0:n], in_=L["mv"][:, 0:n, 1], func=AF.Sqrt,
                                                   bias=EPS1[:, 0:1], scale=1.0),
                     reads=[bLNS[li], bCONST], writes=[bLNSr[li]])
                for j in range(n):
                    S.op("dve", lambda e, j=j: e.scalar_tensor_tensor(
                        out=ZV[j][:], in0=ZV[j][:], scalar=L["mv"][:, j, 0:1], in1=SGB[:, 0, :],
                        op0=ALU.subtract, op1=ALU.mult),
                         reads=[bZV[j], bLNS[li], bPAR], writes=[bZV[j]])
                S.op("dve", lambda e: e.reciprocal(out=L["rstd"][:, 0:n], in_=L["rstd"][:, 0:n]),
                     reads=[bLNSr[li]], writes=[bLNSr[li]])
                for j in range(n):
                    if is_s:
                        S.op("dve", lambda e, j=j: e.scalar_tensor_tensor(
                            out=ZV[j][:], in0=ZV[j][:], scalar=L["rstd"][:, j:j + 1], in1=SGB[:, 1, :],
                            op0=ALU.mult, op1=ALU.add),
                             reads=[bZV[j], bLNSr[li], bPAR], writes=[bZV[j]])
                        S.op("act", lambda e, j=j: e.copy(out=VB[j][:], in_=ZV[j][:]), reads=[bZV[j]], writes=[bVB[j]])
                        S.dma("sp", sguv_d[l], ZV[j][:], reads=[bZV[j]], sem="sout")
                    else:
                        S.op("dve", lambda e, j=j: e.scalar_tensor_tensor(
                            out=VB[j][:], in0=ZV[j][:], scalar=L["rstd"][:, j:j + 1], in1=SGB[:, 1, :],
                            op0=ALU.mult, op1=ALU.add),
                             reads=[bZV[j], bLNSr[li], bPAR], writes=[bVB[j]])

            def partSGU(gi, t0, n, is_s):
                Wm = WTS if is_s else WT
                Bm = BIASS if is_s else BIAS
                mbank = [4, 5, 6]
                for j in range(n):
                    def mms(e, j=j):
                        for h in range(NH):
                            ins = e.matmul(PS[:, mbank[j], h * HD:(h + 1) * HD], lhsT=Wm[:, h, :],
                                           rhs=VB[j][:, h * HD:(h + 1) * HD], start=True, stop=True)
                        return ins
                    S.op("pe", mms, reads=[bPAR, bVB[j]], writes=[bPS[mbank[j]]])
                for j in range(n):
                    S.op("dve", lambda e, j=j: e.tensor_tensor(
                        out=ZV[j][:].rearrange("p (h d) -> p h d", h=NH),
                        in0=PS[:, mbank[j], :].rearrange("p (h d) -> p h d", h=NH),
                        in1=Bm[:].unsqueeze(2).to_broadcast([128, NH, HD]), op=ALU.add),
                         reads=[bPS[mbank[j]], bPAR], writes=[bZV[j]])
                    S.op("dve", lambda e, j=j: e.tensor_tensor(out=SS[j][:], in0=ZV[j][:], in1=U[j][:], op=ALU.mult),
                         reads=[bZV[j], bU[j]], writes=[bSS[j]])
                for j in range(n):
                    pb = psbf(mbank[j])

                    def trss(e, j=j, pb=pb):
                        for c in range(4):
                            ins = e.transpose(out=pb[:, c * 128:(c + 1) * 128],
                                              in_=SS[j][:, c * 128:(c + 1) * 128], identity=IDB[:])
                        return ins
                    S.op("pe", trss, reads=[bSS[j], bCONST], writes=[bPS[mbank[j]]])
                    S.op("act", lambda e, j=j, pb=pb: e.copy(
                        out=STt[:, :, j * 128:(j + 1) * 128],
                        in_=pb[:, 0:512].rearrange("p (c n) -> p c n", c=4)),
                         reads=[bPS[mbank[j]]], writes=[bST])

            def partConv(gi, t0, n, is_s):
                N = n * 128
                for c in range(4):
                    def cv(e, c=c):
                        if is_s:
                            for s_ in range(2):
                                for jt in range(KW):
                                    for i in range(4):
                                        p0 = 32 * i
                                        ins = e.matmul(PS[p0:p0 + 32, c, s_ * 64:(s_ + 1) * 64],
                                                       lhsT=D32[p0:p0 + 32, jt, c, :], rhs=AS[p0:p0 + 32, c, s_, jt:jt + 64],
                                                       start=(jt == 0), stop=(jt == KW - 1), tile_position=(p0, p0))
                            return ins
                        for jt in range(KW):
                            for i in range(4):
                                p0 = 32 * i
                                ins = e.matmul(PS[p0:p0 + 32, c, 0:N], lhsT=D32[p0:p0 + 32, jt, c, :],
                                               rhs=ABF[p0:p0 + 32, c, jt:jt + N],
                                               start=(jt == 0), stop=(jt == KW - 1), tile_position=(p0, p0))
                        return ins
                    S.op("pe", cv, reads=[bABUF, bPAR], writes=[bPS[c]])
                    S.op("act", lambda e, c=c: e.activation(out=ACC[:, c, 0:N], in_=PS[:, c, 0:N], func=AF.Identity,
                                                            bias=CB[:, c:c + 1], scale=1.0),
                         reads=[bPS[c], bPAR], writes=[bACC[c]])

            def partStats(gi, t0, n, is_s):
                N = n * 128
                sqbufs = [(SQ[0], bSQ[0]), (SIL[0], bSIL[0]), (SIL[1], bSIL[1])]
                for c in range(4):
                    sq, bsq = sqbufs[c % 3]
                    S.op("act", lambda e, c=c, sq=sq: e.activation(out=sq[:, 0:N], in_=ACC[:, c, 0:N], func=AF.Square),
                         reads=[bACC[c]], writes=[bsq])
                    S.op("pe", lambda e, c=c: e.matmul(PS[:, 4, 0:N], lhsT=ONESC[:], rhs=ACC[:, c, 0:N],
                                                       start=(c == 0), stop=(c == 3)),
                         reads=[bACC[c], bCONST], writes=[bPS[4]])
                    S.op("pe", lambda e, c=c, sq=sq: e.matmul(PS[:, 5, 0:N], lhsT=ONESC[:], rhs=sq[:, 0:N],
                                                              start=(c == 0), stop=(c == 3)),
                         reads=[bsq, bCONST], writes=[bPS[5]])
                S.op("act", lambda e: e.copy(out=MEANB[:, 0:N], in_=PS[:, 4, 0:N]), reads=[bPS[4]], writes=[bGT[0]])
                S.op("dve", lambda e: e.tensor_tensor(out=RSTDB[:, 0:N], in0=MEANB[:, 0:N], in1=MEANB[:, 0:N],
                                                      op=ALU.mult),
                     reads=[bGT[0]], writes=[bGT[1]])
                S.op("dve", lambda e: e.tensor_tensor(out=RSTDB[:, 0:N], in0=PS[:, 5, 0:N], in1=RSTDB[:, 0:N],
                                                      op=ALU.subtract),
                     reads=[bPS[5], bGT[1]], writes=[bGT[1]])
                S.op("act", lambda e: e.activation(out=RSTDB[:, 0:N], in_=RSTDB[:, 0:N], func=AF.Sqrt,
                                                   bias=EPS1[:, 0:1], scale=1.0),
                     reads=[bGT[1], bCONST], writes=[bGT[1]])
                S.op("dve", lambda e: e.reciprocal(out=RSTDB[:, 0:N], in_=RSTDB[:, 0:N]), reads=[bGT[1]], writes=[bGT[1]])

            def partNorm(gi, t0, n, is_s, chunks=range(4)):
                N = n * 128
                for c in chunks:
                    S.op("dve", lambda e, c=c: e.tensor_tensor(out=ACC[:, c, 0:N], in0=ACC[:, c, 0:N],
                                                               in1=MEANB[:, 0:N], op=ALU.subtract),
                         reads=[bACC[c], bGT[0], bGT[1]], writes=[bACC[c]])
                    S.op("dve", lambda e, c=c: e.tensor_tensor(out=ACC[:, c, 0:N], in0=ACC[:, c, 0:N],
                                                               in1=RSTDB[:, 0:N], op=ALU.mult),
                         reads=[bACC[c], bGT[0], bGT[1]], writes=[bACC[c]])
                    S.op("act", lambda e, c=c: e.activation(out=CT[:, c, 0:N], in_=ACC[:, c, 0:N], func=AF.Silu,
                                                            scale=CLG[:, c:c + 1], bias=CLB[:, c:c + 1]),
                         reads=[bACC[c], bPAR], writes=[bCT])

            def partOut(gi, t0, n, is_s):
                for j in range(n):
                    t = t0 + j
                    mb = 4 + 2 * (rr["mx"] % 2)
                    rr["mx"] += 1

                    def mmo(e, j=j, mb=mb):
                        for half in range(2):
                            Wo = WoutA if half == 0 else WoutB
                            for k in range(8):
                                lhs = CT[:, k, j * 128:(j + 1) * 128] if k < 4 else STt[:, k - 4, j * 128:(j + 1) * 128]
                                ins = e.matmul(PS[:, mb + half, :], lhsT=lhs, rhs=Wo[:, k, :],
                                               start=(k == 0), stop=(k == 7))
                        return ins
                    S.op("pe", mmo, reads=[bRING[sA], bRING[sB], bCT, bST], writes=bb(mb) + bb(mb + 1))
                    msrc = PS[:, mb:mb + 2, :].rearrange("p a n -> p (a n)")
                    S.op("dve", lambda e, t=t, msrc=msrc: e.scalar_tensor_tensor(
                        out=X[:, t, :], in0=X[:, t, :], scalar=ALPHA, in1=msrc, op0=ALU.mult, op1=ALU.add),
                         reads=[bX[t]] + bb(mb) + bb(mb + 1), writes=[bX[t]])
                layer_norm_group([t0 + j for j in range(n)], EPS1)

            partA(0, *groups[0])
            for gi, g in enumerate(groups):
                partB(gi, *g)
                if gi + 1 < len(groups):
                    ensure_xt([groups[gi + 1][0] + j for j in range(groups[gi + 1][1])])
                partConv(gi, *g)
                partSGU(gi, *g)
                partStats(gi, *g)
                if gi + 1 < len(groups):
                    partA(gi + 1, *groups[gi + 1], between=lambda c, gi=gi, g=g: partNorm(gi, *g, chunks=[c]))
                else:
                    partNorm(gi, *g)
                partOut(gi, *g)
                if gi >= 1:
                    ensure_xt([groups[gi - 1][0] + j for j in range(groups[gi - 1][1])])
            load_next()
            load_next()

        for _ in range(NSLOT):
            load_next()
        prep_dma(0)
        for b in range(NB):
            for t in range(NT):
                S.dma("sp", X[:, t, :], xin[(b * NT + t) * 128:(b * NT + t + 1) * 128, :], writes=[bX[t]],
                      sem="xin%d" % (t // 3))
            for t in range(NT):
                i = rr["ln"] % 2
                rr["ln"] += 1
                S.op("act", lambda e, t=t, i=i: e.copy(out=XN[i][:], in_=X[:, t, :]), reads=[bX[t]], writes=[bXN[i]])
                transposes_to_XT(t, i, 0 if t % 2 == 0 else 2)
            for l in range(NL):
                prep_pe(l)
                if dbg_stage == "x0" and l == 0:
                    for t in range(NT):
                        S.dma("sp", dbg_d[(b * NT + t) * 128:(b * NT + t + 1) * 128, :], X[:, t, :], reads=[bX[t]], sem="sout")
                ffn(l, 1, l == NL - 1)
                if dbg_stage == "x1" and l == 0:
                    for t in range(NT):
                        S.dma("sp", dbg_d[(b * NT + t) * 128:(b * NT + t + 1) * 128, :], X[:, t, :], reads=[bX[t]], sem="sout")
                mixer(l, b)
                if dbg_stage == "x2" and l == 0:
                    for t in range(NT):
                        S.dma("sp", dbg_d[(b * NT + t) * 128:(b * NT + t + 1) * 128, :], X[:, t, :], reads=[bX[t]], sem="sout")
                if l + 1 < NL:
                    prep_dma(l + 1)
                elif b + 1 < NB:
                    prep_dma(0)
                ffn(l, 2, l == NL - 1)
            for t in range(NT):
                gt = b * NT + t
                if gt == 0:
                    continue
                S.dma("sp", y_d[(gt - 1) * 128:gt * 128, :], X[:, t, :], reads=[bX[t]], sem="yout")
        S.wait_all("sp", ["yout", "sout"])
        block = es.enter_context(nc.Block())
        S.emit(block)
    return nc


_WNAMES = ["w_ffn1_gate", "w_ffn1_up", "w_ffn1_down", "w_ffn2_gate", "w_ffn2_up", "w_ffn2_down", "w_in", "w_out",
           "ln1_g", "ln1_b", "ln2_g", "ln2_b", "ln3_g", "ln3_b", "conv_k", "conv_b", "conv_ln_g", "conv_ln_b",
           "sgu_ln_g", "sgu_ln_b", "w_sgu", "b_sgu"]


def make_in_maps(inputs, cores=range(NCORES)):
    xp = np.asarray(inputs["x_prompt"], dtype=np.float32)
    xs = np.asarray(inputs["x_sample"], dtype=np.float32)
    cc = np.asarray(inputs["cache_conv"], dtype=np.float32)
    W = {k: np.ascontiguousarray(np.asarray(inputs[k], dtype=np.float32)) for k in _WNAMES}
    maps = []
    for c in cores:
        seq, part = c // 4, c % 4
        s0 = part * OWN
        xin = np.zeros((NB * NT * 128, D), np.float32)
        if part > 0:
            xin[0:128] = xp[seq, s0 - 128:s0]
        xin[128:128 + OWN] = xp[seq, s0:s0 + OWN]
        xin[128 + OWN:] = xs[2 * c:2 * c + 2].reshape(128, D)
        m = {"xin": xin,
             "cmask": np.full((128, 1), 0.0 if part == 0 else 1.0, np.float32),
             "cache": np.ascontiguousarray(cc[:, 2 * c:2 * c + 2])}
        m.update(W)
        maps.append(m)
    return maps


_NC_CACHE = {}


def kernel(**inputs):
    if "nc" not in _NC_CACHE:
        _NC_CACHE["nc"] = build_program(DEPTH)
    nc = _NC_CACHE["nc"]
    maps = make_in_maps(inputs)
    res = run_bass_kernel_spmd(nc, maps, core_ids=list(range(NCORES)))
    R = res.results
    y_prompt = np.zeros((2, 8192, D), np.float32)
    y_sample = np.zeros((16, 64, D), np.float32)
    conv_p = np.zeros((DEPTH, 2, CTXN, DC), np.float32)
    conv_s = np.zeros((DEPTH, 16, CTXN, DC), np.float32)
    sgu_v = np.zeros((DEPTH, 16, 64, DC), np.float32)
    for c in range(NCORES):
        seq, part = c // 4, c % 4
        y = np.asarray(R[c]["y"])
        y_prompt[seq, part * OWN:(part + 1) * OWN] = y[0:OWN]
        y_sample[2 * c:2 * c + 2] = y[OWN:].reshape(2, 64, D)
        if part == 3:
            conv_p[:, seq] = np.asarray(R[c]["conv_p"])
        conv_s[:, 2 * c:2 * c + 2] = np.asarray(R[c]["conv_s"])
        sgu_v[:, 2 * c:2 * c + 2] = np.asarray(R[c]["sgu_v"]).reshape(DEPTH, 2, 64, DC)
    return (y_prompt, y_sample, conv_p, conv_s, sgu_v)
```

```python
import numpy as np
import concourse.bass as bass
import concourse.mybir as mybir
from concourse.bass_utils import run_bass_kernel_spmd
from contextlib import ExitStack

F32 = mybir.dt.float32
BF16 = mybir.dt.bfloat16
AF = mybir.ActivationFunctionType
ALU = mybir.AluOpType

D = 1024
DFF = 2816
NF = 22
DEPTH = 4
DC = 512
NH = 8
HD = 64
KW = 31
CTXN = 30
NB = 2
NT = 9
NCORES = 8
OWN = 2048
ALPHA = float((2 * DEPTH) ** 0.25)
EPS = 1e-5
FG = 4
FGROUPS = [(0, 2), (2, 4), (6, 4), (10, 4), (14, 4), (18, 4)]
NSLOT = 3
SLOTW = 12288


class Buf:
    __slots__ = ("name", "w", "r")

    def __init__(self, name):
        self.name = name
        self.w = None
        self.r = []


class Sched:
    COMPUTE = ("pe", "act", "dve", "pool")

    def __init__(self, nc, es):
        self.nc = nc
        self.es = es
        self.eng = {"pe": nc.tensor, "act": nc.scalar, "dve": nc.vector, "pool": nc.gpsimd, "sp": nc.sync}
        self.sems = {}
        self.cnt = {}
        self.isdma = {}
        for e in self.COMPUTE:
            self.sems[e] = es.enter_context(nc.semaphore("tk_" + e))
            self.cnt[e] = 0
            self.isdma[e] = False
        self.prog = {e: [] for e in self.eng}
        self.waited = {e: {} for e in self.eng}

    def dma_sem(self, name):
        if name not in self.sems:
            self.sems[name] = self.es.enter_context(self.nc.semaphore("dq_" + name))
            self.cnt[name] = 0
            self.isdma[name] = True
        return name

    def _deps(self, eng, reads, writes):
        deps = {}

        def add(ev):
            if ev is None:
                return
            k, v = ev
            if deps.get(k, 0) < v:
                deps[k] = v
        for b in reads:
            add(b.w)
        for b in writes:
            add(b.w)
            for r in b.r:
                add(r)
        waits = []
        for k, v in deps.items():
            if k == "pe" and eng == "pe":
                continue
            if self.isdma[k]:
                v = self.cnt[k]
            if self.waited[eng].get(k, 0) >= v:
                continue
            self.waited[eng][k] = v
            waits.append((k, v))
        return waits

    def _commit(self, ev, reads, writes):
        for b in reads:
            b.r.append(ev)
            if len(b.r) > 64:
                m = {}
                for k, v in b.r:
                    if m.get(k, 0) < v:
                        m[k] = v
                b.r = list(m.items())
        for b in writes:
            b.w = ev
            b.r = []

    def op(self, eng, fn, reads=(), writes=()):
        waits = self._deps(eng, reads, writes)
        self.cnt[eng] += 1
        ev = (eng, self.cnt[eng])
        self._commit(ev, reads, writes)
        self.prog[eng].append((waits, fn, (eng, 1)))
        return ev

    def dma(self, q, out, in_, reads=(), writes=(), sem="misc", **kw):
        self.dma_sem(sem)
        waits = self._deps(q, reads, writes)
        self.cnt[sem] += 16
        ev = (sem, self.cnt[sem])
        self._commit(ev, reads, writes)
        self.prog[q].append((waits, lambda e: e.dma_start(out=out, in_=in_, **kw), (sem, 16)))
        return ev

    def wait_all(self, eng, semnames):
        waits = [(k, self.cnt[k]) for k in semnames if self.cnt.get(k, 0) > 0]
        self.prog[eng].append((waits, None, None))

    def emit(self, block):
        def run(engname):
            def body(e):
                for waits, fn, inc in self.prog[engname]:
                    for k, v in waits:
                        e.wait_ge(self.sems[k], v)
                    if fn is not None:
                        ins = fn(e)
                        ins.then_inc(self.sems[inc[0]], inc[1])
            return body
        block.tensor(run("pe"))
        block.scalar(run("act"))
        block.vector(run("dve"))
        block.gpsimd(run("pool"))
        block.sync(run("sp"))


def build_program(NL=DEPTH, dbg_stage=None):
    nc = bass.Bass("TRN2", target_bir_lowering=False)
    TOK = NB * NT * 128

    def din(name, shape):
        return nc.dram_tensor(name, list(shape), F32, kind="ExternalInput").ap()

    def dout(name, shape):
        return nc.dram_tensor(name, list(shape), F32, kind="ExternalOutput").ap()

    xin = din("xin", [TOK, D])
    cmask_d = din("cmask", [128, 1])
    cache_d = din("cache", [DEPTH, 2, CTXN, DC])
    wd = {}
    for nm, shp in [("w_ffn1_gate", [DEPTH, D, DFF]), ("w_ffn1_up", [DEPTH, D, DFF]), ("w_ffn1_down", [DEPTH, DFF, D]),
                    ("w_ffn2_gate", [DEPTH, D, DFF]), ("w_ffn2_up", [DEPTH, D, DFF]), ("w_ffn2_down", [DEPTH, DFF, D]),
                    ("w_in", [DEPTH, D, 2048]), ("w_out", [DEPTH, D, D]),
                    ("ln1_g", [DEPTH, D]), ("ln1_b", [DEPTH, D]), ("ln2_g", [DEPTH, D]), ("ln2_b", [DEPTH, D]),
                    ("ln3_g", [DEPTH, D]), ("ln3_b", [DEPTH, D]),
                    ("conv_k", [DEPTH, KW, DC]), ("conv_b", [DEPTH, DC]), ("conv_ln_g", [DEPTH, DC]),
                    ("conv_ln_b", [DEPTH, DC]), ("sgu_ln_g", [DEPTH, DC]), ("sgu_ln_b", [DEPTH, DC]),
                    ("w_sgu", [DEPTH, NH, 128, 128]), ("b_sgu", [DEPTH, NH, 128])]:
        wd[nm] = din(nm, shp)
    y_d = dout("y", [TOK - 128, D])
    convp_d = dout("conv_p", [DEPTH, CTXN, DC])
    convs_d = dout("conv_s", [DEPTH, 2, CTXN, DC])
    sguv_d = dout("sgu_v", [DEPTH, 128, DC])
    dbg_d = dout("dbg", [TOK, D]) if dbg_stage else None

    es = ExitStack()
    with es:
        es.enter_context(nc.allow_non_contiguous_dma(reason="small strided parameter loads"))
        S = Sched(nc, es)

        def sb(name, shape, dt=F32):
            return es.enter_context(nc.sbuf_tensor(name, list(shape), dt))

        X = sb("X", [128, NT, D])
        XT = sb("XT", [128, 8, NT * 128], BF16)
        RING = [sb("ring%d" % i, [128, SLOTW], BF16) for i in range(NSLOT)]
        GT = [sb("gt%d" % i, [128, FG, 384], BF16) for i in range(2)]
        SIL = [sb("sil%d" % i, [128, 384]) for i in range(2)]
        XN = [sb("xn%d" % i, [128, D], BF16) for i in range(2)]
        LNG = sb("lng", [128, 2, D])
        LNS = [dict(st=sb("lnst%d" % i, [128, 3, 2, 6]), mv=sb("lnmv%d" % i, [128, 3, 2]),
                    rstd=sb("lnrs%d" % i, [128, 3]), nmr=sb("lnnm%d" % i, [128, 3])) for i in range(2)]
        ABF = sb("abf", [128, 4, CTXN + 384], BF16)
        A30 = sb("a30", [128, 4, 2, CTXN])
        D32 = sb("d32", [128, KW, 4, 32], BF16)
        M32 = sb("m32", [128, 32])
        ACC = sb("acc", [128, 4, 384])
        SQ0 = sb("sq0", [128, 384])
        SQ = [SQ0, SQ0]
        CT = sb("ct", [128, 4, 384], BF16)
        STt = sb("stt", [128, 4, 384], BF16)
        U = [sb("u%d" % i, [128, DC]) for i in range(3)]
        ZV = [sb("zv%d" % i, [128, DC]) for i in range(3)]
        VB = [sb("vb%d" % i, [128, DC], BF16) for i in range(3)]
        SS = VB
        MEANB = GT[0][:].rearrange("p f n -> p (f n)").bitcast(F32)[:, 0:384]
        RSTDB = GT[1][:].rearrange("p f n -> p (f n)").bitcast(F32)[:, 0:384]
        SGB = sb("sgb", [128, 2, DC])
        WT = sb("wt", [128, NH, 128], BF16)
        WTS = sb("wts", [128, NH, 128], BF16)
        WNAT = sb("wnat", [128, NH, 128], BF16)
        WNATS = sb("wnats", [128, NH, 128], BF16)
        BIAS = sb("bias", [128, NH])
        BIASS = sb("biass", [128, NH])
        MASKB = sb("maskb", [128, 128], BF16)
        SCR1 = sb("scr1", [124, DC])
        KNAT = SCR1[0:KW, :]
        CNAT = SCR1[64:64 + 2 * CTXN, :]
        KCOL = sb("kcol", [128, 4, 32])
        CB = sb("cb", [128, 4])
        CLG = sb("clg", [128, 4])
        CLB = sb("clb", [128, 4])
        CTX = sb("ctx", [128, DEPTH, 4, CTXN], BF16)
        CTXS = sb("ctxs", [128, 4, 2, CTXN], BF16)
        CMASK = sb("cmaskt", [128, 1])
        EPS1 = sb("eps1", [128, 1])
        EPS4 = sb("eps4", [128, 1])
        IDF = sb("idf", [128, 128])
        IDB = sb("idb", [128, 128], BF16)
        ONESC = sb("onesc", [128, 128])
        STG = U[2][0:CTXN, :]
        PS = es.enter_context(nc.psum_tensor("psall", [128, 8, 512], F32))

        def psbf(bank):
            return PS[:, bank, :].bitcast(BF16)

        bX = [Buf("X%d" % t) for t in range(NT)]
        bXT = [Buf("XT%d" % t) for t in range(NT)]
        bRING = [Buf("ring%d" % i) for i in range(NSLOT)]
        bGT = [Buf("gt%d" % i) for i in range(2)]
        bSIL = [Buf("sil%d" % i) for i in range(2)]
        bXN = [Buf("xn%d" % i) for i in range(2)]
        bLNG = Buf("lng")
        bLNS = [Buf("lns%d" % i) for i in range(2)]
        bLNSr = [Buf("lnsr%d" % i) for i in range(2)]
        bABUF = Buf("abuf")
        bACC = [Buf("acc%d" % c) for c in range(4)]
        bSQ0 = Buf("sq0")
        bSQ = [bSQ0, bSQ0]
        bCT = Buf("ct")
        bST = Buf("stt")
        bU = [Buf("u%d" % i) for i in range(3)]
        bZV = [Buf("zv%d" % i) for i in range(3)]
        bVB = [Buf("vb%d" % i) for i in range(3)]
        bSS = bVB
        bPAR = Buf("layer_params")
        bWNAT = Buf("wnat")
        bKNAT = Buf("knat")
        bCNAT = Buf("cnat")
        bCTX = [Buf("ctx%d" % l) for l in range(DEPTH)]
        bCONST = Buf("const")
        bSTG = bU[2]
        bA30 = Buf("a30")
        bPS = [Buf("ps%d" % i) for i in range(8)]
        bPS7h = [bPS[7]]

        def bb(bank):
            return [bPS[bank]]

        S.op("pool", lambda e: e.memset(IDF[:], 0.0), writes=[bCONST])
        S.op("pool", lambda e: e.affine_select(out=IDF[:], in_=IDF[:], pattern=[[-1, 128]], compare_op=ALU.not_equal,
                                               fill=1.0, base=0, channel_multiplier=1), reads=[bCONST], writes=[bCONST])
        S.op("pool", lambda e: e.memset(ONESC[:], 1.0), writes=[bCONST])
        S.op("pool", lambda e: e.affine_select(out=ONESC[:], in_=ONESC[:], pattern=[[1, 128]], compare_op=ALU.is_ge,
                                               fill=0.0, base=0, channel_multiplier=-1), reads=[bCONST], writes=[bCONST])

        def c_misc(e):
            e.memset(EPS1[:], EPS)
            e.memset(EPS4[:], 4.0 * EPS)
            e.memset(CTX[:], 0.0)
            e.memset(KCOL[:], 0.0)
            return e.memset(WNATS[:], 0.0)
        S.op("pool", c_misc, writes=[bCONST, bWNAT, bPAR] + bCTX)
        S.op("act", lambda e: e.copy(out=IDB[:], in_=IDF[:]), reads=[bCONST], writes=[bCONST])
        S.op("act", lambda e: e.copy(out=MASKB[:], in_=ONESC[:]), reads=[bCONST], writes=[bCONST])
        S.op("dve", lambda e: e.memset(ONESC[:], 1.0 / DC), reads=[bCONST], writes=[bCONST])
        S.op("dve", lambda e: e.tensor_tensor(out=M32[:], in0=IDF[:, 0:32], in1=IDF[:, 32:64], op=ALU.add),
             reads=[bCONST], writes=[bCONST])
        S.op("dve", lambda e: e.tensor_tensor(out=M32[:], in0=M32[:], in1=IDF[:, 64:96], op=ALU.add),
             reads=[bCONST], writes=[bCONST])
        S.op("dve", lambda e: e.tensor_tensor(out=M32[:], in0=M32[:], in1=IDF[:, 96:128], op=ALU.add),
             reads=[bCONST], writes=[bCONST])
        S.dma("sp", CMASK[:], cmask_d, writes=[bCONST], sem="cst")

        units = []
        for b in range(NB):
            for l in range(NL):
                for g in range(len(FGROUPS)):
                    units.append(("F", l, 1, g))
                units.append(("MA", l))
                units.append(("MB", l))
                for g in range(len(FGROUPS)):
                    units.append(("F", l, 2, g))
        ring_state = {"next": 0}

        def load_next():
            u = ring_state["next"]
            if u >= len(units):
                return
            ring_state["next"] = u + 1
            slot = u % NSLOT
            R = RING[slot]
            sem = "ring%d" % slot
            un = units[u]
            if un[0] == "F":
                _, l, which, g = un
                f0, n = FGROUPS[g]
                wg = wd["w_ffn%d_gate" % which]
                wu = wd["w_ffn%d_up" % which]
                wdn = wd["w_ffn%d_down" % which]
                S.dma("pool", R[:, 0:8 * n * 128].rearrange("p (k n) -> p k n", k=8),
                      wg[l, :, f0 * 128:(f0 + n) * 128].rearrange("(k p) n -> p k n", p=128),
                      writes=[bRING[slot]], sem=sem)
                S.dma("pool", R[:, 4096:4096 + 8 * n * 128].rearrange("p (k n) -> p k n", k=8),
                      wu[l, :, f0 * 128:(f0 + n) * 128].rearrange("(k p) n -> p k n", p=128),
                      writes=[bRING[slot]], sem=sem)
                S.dma("pool", R[:, 8192:8192 + n * D].rearrange("p (f n) -> p f n", f=n),
                      wdn[l, f0 * 128:(f0 + n) * 128, :].rearrange("(f p) n -> p f n", p=128),
                      writes=[bRING[slot]], sem=sem)
            else:
                l = un[1]
                c0 = 0 if un[0] == "MA" else 1024
                o0 = 0 if un[0] == "MA" else 512
                S.dma("pool", R[:, 0:8192].rearrange("p (k n) -> p k n", k=8),
                      wd["w_in"][l, :, c0:c0 + 1024].rearrange("(k p) n -> p k n", p=128),
                      writes=[bRING[slot]], sem=sem)
                S.dma("pool", R[:, 8192:12288].rearrange("p (k n) -> p k n", k=8),
                      wd["w_out"][l, :, o0:o0 + 512].rearrange("(k p) n -> p k n", p=128),
                      writes=[bRING[slot]], sem=sem)

        cur_unit = {"i": 0}

        def take_unit():
            u = cur_unit["i"]
            cur_unit["i"] = u + 1
            return u % NSLOT

        rr = {"ln": 0, "trb": 0, "gu": 0, "y": 0, "gtb": 0, "mx": 0, "bt": 0, "sq": 0}
        deferred = {}

        def ensure_xt(tiles):
            for t in tiles:
                fn = deferred.pop(t, None)
                if fn is not None:
                    fn()

        def flush_deferred():
            ensure_xt(sorted(deferred.keys()))

        def transposes_to_XT(t, xn_i, bank):
            pb = psbf(bank)

            def tr(e):
                for k in range(8):
                    ins = e.transpose(out=pb[:, k * 128:(k + 1) * 128], in_=XN[xn_i][:, k * 128:(k + 1) * 128],
                                      identity=IDB[:])
                return ins
            S.op("pe", tr, reads=[bXN[xn_i], bCONST], writes=[bPS[bank]])
            S.op("act", lambda e: e.copy(out=XT[:, :, t * 128:(t + 1) * 128],
                                         in_=pb.rearrange("p (k n) -> p k n", k=8)),
                 reads=[bPS[bank]], writes=[bXT[t]])

        def layer_norm_group(tiles, eps_tile, need_xt=True):
            i = rr["ln"] % 2
            rr["ln"] += 1
            L = LNS[i]
            n = len(tiles)
            for k, t in enumerate(tiles):
                def stats(e, k=k, t=t):
                    e.bn_stats(out=L["st"][:, k, 0, :], in_=X[:, t, 0:512])
                    return e.bn_stats(out=L["st"][:, k, 1, :], in_=X[:, t, 512:1024])
                S.op("dve", stats, reads=[bX[t]], writes=[bLNS[i], bLNSr[i]])
            for k, t in enumerate(tiles):
                S.op("dve", lambda e, k=k: e.bn_aggr(out=L["mv"][:, k, :], in_=L["st"][:, k, :, :]),
                     reads=[bLNS[i]], writes=[bLNS[i]])
            S.op("act", lambda e: e.activation(out=L["rstd"][:, 0:n], in_=L["mv"][:, 0:n, 1], func=AF.Sqrt,
                                               bias=eps_tile[:, 0:1], scale=1.0),
                 reads=[bLNS[i], bCONST], writes=[bLNSr[i]])
            for k, t in enumerate(tiles):
                S.op("dve", lambda e, k=k, t=t: e.scalar_tensor_tensor(
                    out=X[:, t, :], in0=X[:, t, :], scalar=L["mv"][:, k, 0:1], in1=LNG[:, 0, :],
                    op0=ALU.subtract, op1=ALU.mult),
                     reads=[bX[t], bLNS[i], bLNG], writes=[bX[t]])
            S.op("dve", lambda e: e.reciprocal(out=L["rstd"][:, 0:n], in_=L["rstd"][:, 0:n]),
                 reads=[bLNSr[i]], writes=[bLNSr[i]])
            for k, t in enumerate(tiles):
                S.op("dve", lambda e, k=k, t=t: e.scalar_tensor_tensor(
                    out=X[:, t, :], in0=X[:, t, :], scalar=L["rstd"][:, k:k + 1], in1=LNG[:, 1, :],
                    op0=ALU.mult, op1=ALU.add),
                     reads=[bX[t], bLNSr[i], bLNG], writes=[bX[t]])
            if need_xt:
                for t in tiles:
                    def later(t=t):
                        xi = rr["trb"] % 2
                        rr["trb"] += 1
                        S.op("act", lambda e: e.copy(out=XN[xi][:], in_=X[:, t, :]), reads=[bX[t]], writes=[bXN[xi]])
                        transposes_to_XT(t, xi, 0 if xi == 0 else 2)
                    deferred[t] = later

        def load_lng(gname, bname, l):
            S.dma("sp", LNG[:, 0, :], wd[gname][l:l + 1, :].partition_broadcast(128), writes=[bLNG], sem="lng")
            S.dma("sp", LNG[:, 1, :], wd[bname][l:l + 1, :].partition_broadcast(128), writes=[bLNG], sem="lng")

        def ffn(l, which, last_layer):
            load_lng("ln1_g" if which == 1 else "ln3_g", "ln1_b" if which == 1 else "ln3_b", l)
            slots = {}
            sched = [(g, tg) for g in range(len(FGROUPS)) for tg in range(NT // 3)]

            def slot_of(g):
                if g not in slots:
                    slots[g] = take_unit()
                return slots[g]

            gt_of = {}

            def P1(idx):
                g, tg = sched[idx]
                f0, n = FGROUPS[g]
                slot = slot_of(g)
                R = RING[slot]
                Wg = R[:, 0:8 * n * 128].rearrange("p (k n) -> p k n", k=8)
                Wu = R[:, 4096:4096 + 8 * n * 128].rearrange("p (k n) -> p k n", k=8)
                gi = rr["gtb"] % 2
                rr["gtb"] += 1
                gt_of[idx] = gi
                c0 = tg * 384
                ensure_xt([tg * 3 + j for j in range(3)])
                xbufs = [bXT[tg * 3 + j] for j in range(3)]
                for fi in range(n):
                    bp = rr["gu"] % 2
                    rr["gu"] += 1
                    gb, ub = 2 * bp, 2 * bp + 1

                    def mm(e, W=None, bank=0, fi=fi):
                        for k in range(8):
                            ins = e.matmul(PS[:, bank, 0:384], lhsT=W[:, k, fi * 128:(fi + 1) * 128],
                                           rhs=XT[:, k, c0:c0 + 384], start=(k == 0), stop=(k == 7))
                        return ins
                    S.op("pe", lambda e, mm=mm, gb=gb: mm(e, Wg, gb), reads=[bRING[slot]] + xbufs, writes=[bPS[gb]])
                    S.op("pe", lambda e, mm=mm, ub=ub: mm(e, Wu, ub), reads=[bRING[slot]] + xbufs, writes=[bPS[ub]])
                    S.op("act", lambda e, gb=gb, bp=bp: e.activation(out=SIL[bp][:], in_=PS[:, gb, 0:384], func=AF.Silu),
                         reads=[bPS[gb]], writes=[bSIL[bp]])
                    S.op("dve", lambda e, ub=ub, bp=bp, fi=fi, gi=gi: e.tensor_tensor(
                        out=GT[gi][:, fi, :], in0=SIL[bp][:], in1=PS[:, ub, 0:384], op=ALU.mult),
                         reads=[bSIL[bp], bPS[ub]], writes=[bGT[gi]])

            def P2(idx):
                g, tg = sched[idx]
                f0, n = FGROUPS[g]
                slot = slot_of(g)
                R = RING[slot]
                Wd_ = R[:, 8192:8192 + n * D].rearrange("p (f n) -> p f n", f=n)
                gi = gt_of[idx]
                for j in range(3):
                    t = tg * 3 + j
                    yb = 4 + 2 * (rr["y"] % 2)
                    rr["y"] += 1

                    def mm(e, j=j, yb=yb):
                        for half in range(2):
                            for fi in range(n):
                                ins = e.matmul(PS[:, yb + half, :], lhsT=GT[gi][:, fi, j * 128:(j + 1) * 128],
                                               rhs=Wd_[:, fi, half * 512:(half + 1) * 512],
                                               start=(fi == 0), stop=(fi == n - 1))
                        return ins
                    S.op("pe", mm, reads=[bRING[slot], bGT[gi]], writes=bb(yb) + bb(yb + 1))
                    ysrc = PS[:, yb:yb + 2, :].rearrange("p a n -> p (a n)")
                    if g == 0:
                        S.op("dve", lambda e, t=t, ysrc=ysrc: e.scalar_tensor_tensor(
                            out=X[:, t, :], in0=X[:, t, :], scalar=2.0 * ALPHA, in1=ysrc, op0=ALU.mult, op1=ALU.add),
                             reads=[bX[t]] + bb(yb) + bb(yb + 1), writes=[bX[t]])
                    else:
                        S.op("dve", lambda e, t=t, ysrc=ysrc: e.tensor_tensor(
                            out=X[:, t, :], in0=X[:, t, :], in1=ysrc, op=ALU.add),
                             reads=[bX[t]] + bb(yb) + bb(yb + 1), writes=[bX[t]])
                if g == len(FGROUPS) - 1:
                    if tg >= 1:
                        ensure_xt([(tg - 1) * 3 + j for j in range(3)])
                    layer_norm_group([tg * 3 + j for j in range(3)], EPS4, need_xt=not (last_layer and which == 2))

            P1(0)
            for idx in range(len(sched)):
                if idx + 1 < len(sched):
                    P1(idx + 1)
                P2(idx)
                g, tg = sched[idx]
                if g == len(FGROUPS) - 1 and idx + 1 < len(sched):
                    pass
                if tg == NT // 3 - 1:
                    load_next()
            if last_layer and which == 2:
                flush_deferred()

        def prep_dma(l):
            S.dma("sp", KNAT, wd["conv_k"][l], writes=[bKNAT], sem="par")
            S.dma("sp", CNAT, cache_d[l].rearrange("b r c -> (b r) c"), writes=[bCNAT], sem="par")
            S.dma("sp", CB[:], wd["conv_b"][l].rearrange("(c p) -> p c", p=128), writes=[bPAR], sem="par")
            S.dma("sp", CLG[:], wd["conv_ln_g"][l].rearrange("(c p) -> p c", p=128), writes=[bPAR], sem="par")
            S.dma("sp", CLB[:], wd["conv_ln_b"][l].rearrange("(c p) -> p c", p=128), writes=[bPAR], sem="par")
            S.dma("sp", SGB[:, 0, :], wd["sgu_ln_g"][l:l + 1, :].partition_broadcast(128), writes=[bPAR], sem="par")
            S.dma("sp", SGB[:, 1, :], wd["sgu_ln_b"][l:l + 1, :].partition_broadcast(128), writes=[bPAR], sem="par")
            S.dma("sp", BIAS[:], wd["b_sgu"][l].rearrange("h t -> t h"), writes=[bPAR], sem="par")
            S.dma("sp", BIASS[0:64, :], wd["b_sgu"][l, :, 0:64].rearrange("h t -> t h"), writes=[bPAR], sem="par")
            S.dma("sp", BIASS[64:128, :], wd["b_sgu"][l, :, 0:64].rearrange("h t -> t h"), writes=[bPAR], sem="par")
            S.dma("pool", WNAT[:], wd["w_sgu"][l].rearrange("h t s -> t h s"), writes=[bWNAT], sem="wn")
            S.dma("pool", WNATS[0:64, :, 0:64], wd["w_sgu"][l, :, 0:64, 0:64].rearrange("h t s -> t h s"),
                  writes=[bWNAT], sem="wn")
            S.dma("pool", WNATS[64:128, :, 64:128], wd["w_sgu"][l, :, 0:64, 0:64].rearrange("h t s -> t h s"),
                  writes=[bWNAT], sem="wn")

        def prep_pe(l):
            pk = PS[:, 7, 0:128].rearrange("p (c j) -> p c j", c=4)

            def trk(e):
                for c in range(4):
                    ins = e.transpose(out=pk[:, c, 0:KW], in_=SCR1[0:KW, c * 128:(c + 1) * 128],
                                      identity=IDF[0:KW, 0:KW])
                return ins
            S.op("pe", trk, reads=[bKNAT, bCONST], writes=bPS7h)
            S.op("act", lambda e: e.copy(out=KCOL[:, :, 0:KW], in_=pk[:, :, 0:KW]), reads=bPS7h, writes=[bPAR])
            for c in range(4):
                S.op("dve", lambda e, c=c: e.scalar_tensor_tensor(
                    out=D32[:, :, c, :], in0=M32[:].unsqueeze(1).to_broadcast([128, KW, 32]), scalar=0.5,
                    in1=KCOL[:, c, 0:KW].unsqueeze(2).to_broadcast([128, KW, 32]), op0=ALU.mult, op1=ALU.mult),
                     reads=[bPAR, bCONST], writes=[bPAR])
            pc = PS[:, 7, 0:256].rearrange("p (c j) -> p c j", c=4)

            def trc(e):
                for c in range(4):
                    ins = e.transpose(out=pc[:, c, 0:2 * CTXN], in_=SCR1[64:64 + 2 * CTXN, c * 128:(c + 1) * 128],
                                      identity=IDF[64:64 + 2 * CTXN, 64:64 + 2 * CTXN])
                return ins
            S.op("pe", trc, reads=[bCNAT, bCONST], writes=bPS7h)
            S.op("act", lambda e: e.activation(out=CTXS[:].rearrange("p c s j -> p c (s j)"), in_=pc[:, :, 0:2 * CTXN],
                                               func=AF.Identity, scale=2.0),
                 reads=bPS7h, writes=[bPAR])
            for (src, dst) in ((WNAT, WT), (WNATS, WTS)):
                pb = psbf(7)

                def trw(e, src=src):
                    for h in range(NH):
                        ins = e.transpose(out=pb[:, h * 128:(h + 1) * 128], in_=src[:, h, :], identity=IDB[:])
                    return ins
                S.op("pe", trw, reads=[bWNAT, bCONST], writes=bPS7h)
                S.op("dve", lambda e, dst=dst, pb=pb: e.tensor_tensor(
                    out=dst[:], in0=pb.rearrange("p (h t) -> p h t", h=NH),
                    in1=MASKB[:].unsqueeze(1).to_broadcast([128, NH, 128]), op=ALU.mult),
                     reads=bPS7h + [bCONST], writes=[bPAR])

        def mixer(l, b):
            load_lng("ln2_g", "ln2_b", l)
            sA = take_unit()
            sB = take_unit()
            WinA = RING[sA][:, 0:8192].rearrange("p (k n) -> p k n", k=8)
            WoutA = RING[sA][:, 8192:12288].rearrange("p (k n) -> p k n", k=8)
            WinB = RING[sB][:, 0:8192].rearrange("p (k n) -> p k n", k=8)
            WoutB = RING[sB][:, 8192:12288].rearrange("p (k n) -> p k n", k=8)
            if b == 0:
                groups = [(0, 3, False), (3, 3, False), (6, 3, False)]
            else:
                groups = [(0, 3, False), (3, 3, False), (6, 2, False), (8, 1, True)]
            AS = ABF[:, :, 0:188].rearrange("p c (s j) -> p c s j", s=2)

            def partA(gi, t0, n, is_s, between=None):
                N = n * 128
                c0 = t0 * 128
                ensure_xt([t0 + j for j in range(n)])
                xbufs = [bXT[t0 + j] for j in range(n)]
                want_state = is_s or (b == NB - 1 and gi == len(groups) - 2)
                if is_s:
                    S.op("act", lambda e: e.copy(out=AS[:, :, :, 0:CTXN], in_=CTXS[:]), reads=[bPAR], writes=[bABUF])
                else:
                    S.op("act", lambda e: e.copy(out=ABF[:, :, 0:CTXN], in_=CTX[:, l, :, :]),
                         reads=[bCTX[l]], writes=[bABUF])
                for c in range(4):
                    vb, gb = (2 * c) % 4, (2 * c + 1) % 4

                    def mm(e, col0, bank):
                        for k in range(8):
                            ins = e.matmul(PS[:, bank, 0:N], lhsT=WinA[:, k, col0:col0 + 128],
                                           rhs=XT[:, k, c0:c0 + N], start=(k == 0), stop=(k == 7))
                        return ins
                    S.op("pe", lambda e, mm=mm, c=c, vb=vb: mm(e, c * 128, vb), reads=[bRING[sA]] + xbufs, writes=[bPS[vb]])
                    S.op("pe", lambda e, mm=mm, c=c, gb=gb: mm(e, 512 + c * 128, gb), reads=[bRING[sA]] + xbufs,
                         writes=[bPS[gb]])
                    si = c % 2
                    S.op("act", lambda e, gb=gb, si=si: e.activation(out=SIL[si][:, 0:N], in_=PS[:, gb, 0:N],
                                                                     func=AF.Tanh, scale=0.5),
                         reads=[bPS[gb]], writes=[bSIL[si]])
                    if is_s:
                        S.op("dve", lambda e, c=c, vb=vb, si=si: e.scalar_tensor_tensor(
                            out=AS[:, c, :, CTXN:CTXN + 64],
                            in0=SIL[si][:, 0:128].rearrange("p (s j) -> p s j", s=2), scalar=1.0,
                            in1=PS[:, vb, 0:128].rearrange("p (s j) -> p s j", s=2), op0=ALU.add, op1=ALU.mult),
                             reads=[bPS[vb], bSIL[si]], writes=[bABUF])
                        S.op("dve", lambda e, c=c, vb=vb, si=si: e.scalar_tensor_tensor(
                            out=A30[:, c, :, :],
                            in0=SIL[si][:, 0:128].rearrange("p (s j) -> p s j", s=2)[:, :, 34:64], scalar=1.0,
                            in1=PS[:, vb, 0:128].rearrange("p (s j) -> p s j", s=2)[:, :, 34:64],
                            op0=ALU.add, op1=ALU.mult),
                             reads=[bPS[vb], bSIL[si]], writes=[bA30])
                    else:
                        first_halo = (b == 0 and gi == 0)

                        S.op("dve", lambda e, c=c, vb=vb, si=si: e.scalar_tensor_tensor(
                            out=ABF[:, c, CTXN:CTXN + N], in0=SIL[si][:, 0:N], scalar=1.0, in1=PS[:, vb, 0:N],
                            op0=ALU.add, op1=ALU.mult),
                             reads=[bPS[vb], bSIL[si]], writes=[bABUF])
                        if first_halo:
                            S.op("dve", lambda e, c=c: e.tensor_scalar(
                                out=ABF[:, c, CTXN:CTXN + 128], in0=ABF[:, c, CTXN:CTXN + 128], scalar1=CMASK[:, 0:1],
                                scalar2=None, op0=ALU.mult),
                                 reads=[bABUF, bCONST], writes=[bABUF])
                        if want_state:
                            S.op("dve", lambda e, c=c, vb=vb, si=si: e.scalar_tensor_tensor(
                                out=A30[:, c, 0, :], in0=SIL[si][:, N - CTXN:N], scalar=1.0, in1=PS[:, vb, N - CTXN:N],
                                op0=ALU.add, op1=ALU.mult),
                                 reads=[bPS[vb], bSIL[si]], writes=[bA30])
                    if between is not None:
                        between(c)
                if not is_s:
                    S.op("act", lambda e: e.copy(out=CTX[:, l, :, :], in_=ABF[:, :, N:N + CTXN]),
                         reads=[bABUF], writes=[bCTX[l]])
                if want_state:
                    nseq = 2 if is_s else 1
                    for s_ in range(nseq):
                        def trs(e, s_=s_):
                            for c in range(4):
                                ins = e.transpose(out=PS[0:CTXN, 7, c * 128:(c + 1) * 128], in_=A30[:, c, s_, :],
                                                  identity=IDF[:])
                            return ins
                        S.op("pe", trs, reads=[bA30, bCONST], writes=bPS7h)
                        S.op("act", lambda e: e.activation(out=STG, in_=PS[0:CTXN, 7, :], func=AF.Identity, scale=0.5),
                             reads=bPS7h, writes=[bSTG])
                        dst = convs_d[l, s_] if is_s else convp_d[l]
                        S.dma("sp", dst, STG, reads=[bSTG], sem="sout")

            def partB(gi, t0, n, is_s):
                li = rr["ln"] % 2
                rr["ln"] += 1
                L = LNS[li]
                for j in range(n):
                    t = t0 + j
                    hb = 4 + 2 * (j % 2)

                    def mmb(e, t=t, hb=hb):
                        for half in range(2):
                            for k in range(8):
                                ins = e.matmul(PS[:, hb + half, :], lhsT=XT[:, k, t * 128:(t + 1) * 128],
                                               rhs=WinB[:, k, half * 512:(half + 1) * 512],
                                               start=(k == 0), stop=(k == 7))
                        return ins
                    wr = bb(hb) + bb(hb + 1)
                    S.op("pe", mmb, reads=[bRING[sB], bXT[t]], writes=wr)
                    S.op("act", lambda e, j=j, hb=hb: e.activation(out=U[j][:], in_=PS[:, hb, :], func=AF.Gelu),
                         reads=[bPS[hb]], writes=[bU[j]])
                    S.op("act", lambda e, j=j, hb=hb: e.activation(out=ZV[j][:], in_=PS[:, hb + 1, :], func=AF.Gelu),
                         reads=bb(hb + 1), writes=[bZV[j]])
                for j in range(n):
                    S.op("dve", lambda e, j=j: e.bn_stats(out=L["st"][:, j, 0, :], in_=ZV[j][:]),
                         reads=[bZV[j]], writes=[bLNS[li], bLNSr[li]])
                for j in range(n):
                    S.op("dve", lambda e, j=j: e.bn_aggr(out=L["mv"][:, j, :], in_=L["st"][:, j, 0:1, :]),
                         reads=[bLNS[li]], writes=[bLNS[li]])
                S.op("act", lambda e: e.activation(out=L["rstd"][:, 0:n], in_=L["mv"][:, 0:n, 1], func=AF.Sqrt,
                                                   bias=EPS1[:, 0:1], scale=1.0),
                     reads=[bLNS[li], bCONST], writes=[bLNSr[li]])
                for j in range(n):
                    S.op("dve", lambda e, j=j: e.scalar_tensor_tensor(
                        out=ZV[j][:], in0=ZV[j][:], scalar=L["mv"][:, j, 0:1], in1=SGB[:, 0, :],
                        op0=ALU.subtract, op1=ALU.mult),
                         reads=[bZV[j], bLNS[li], bPAR], writes=[bZV[j]])
                S.op("dve", lambda e: e.reciprocal(out=L["rstd"][:, 0:n], in_=L["rstd"][:, 0:n]),
                     reads=[bLNSr[li]], writes=[bLNSr[li]])
                for j in range(n):
                    if is_s:
                        S.op("dve", lambda e, j=j: e.scalar_tensor_tensor(
                            out=ZV[j][:], in0=ZV[j][:], scalar=L["rstd"][:, j:j + 1], in1=SGB[:, 1, :],
                            op0=ALU.mult, op1=ALU.add),
                             reads=[bZV[j], bLNSr[li], bPAR], writes=[bZV[j]])
                        S.op("act", lambda e, j=j: e.copy(out=VB[j][:], in_=ZV[j][:]), reads=[bZV[j]], writes=[bVB[j]])
                        S.dma("sp", sguv_d[l], ZV[j][:], reads=[bZV[j]], sem="sout")
                    else:
                        S.op("dve", lambda e, j=j: e.scalar_tensor_tensor(
                            out=VB[j][:], in0=ZV[j][:], scalar=L["rstd"][:, j:j + 1], in1=SGB[:, 1, :],
                            op0=ALU.mult, op1=ALU.add),
                             reads=[bZV[j], bLNSr[li], bPAR], writes=[bVB[j]])

            def partSGU(gi, t0, n, is_s):
                Wm = WTS if is_s else WT
                Bm = BIASS if is_s else BIAS
                mbank = [4, 5, 6]
                for j in range(n):
                    def mms(e, j=j):
                        for h in range(NH):
                            ins = e.matmul(PS[:, mbank[j], h * HD:(h + 1) * HD], lhsT=Wm[:, h, :],
                                           rhs=VB[j][:, h * HD:(h + 1) * HD], start=True, stop=True)
                        return ins
                    S.op("pe", mms, reads=[bPAR, bVB[j]], writes=[bPS[mbank[j]]])
                for j in range(n):
                    S.op("dve", lambda e, j=j: e.tensor_tensor(
                        out=ZV[j][:].rearrange("p (h d) -> p h d", h=NH),
                        in0=PS[:, mbank[j], :].rearrange("p (h d) -> p h d", h=NH),
                        in1=Bm[:].unsqueeze(2).to_broadcast([128, NH, HD]), op=ALU.add),
                         reads=[bPS[mbank[j]], bPAR], writes=[bZV[j]])
                    S.op("dve", lambda e, j=j: e.tensor_tensor(out=SS[j][:], in0=ZV[j][:], in1=U[j][:], op=ALU.mult),
                         reads=[bZV[j], bU[j]], writes=[bSS[j]])
                for j in range(n):
                    pb = psbf(mbank[j])

                    def trss(e, j=j, pb=pb):
                        for c in range(4):
                            ins = e.transpose(out=pb[:, c * 128:(c + 1) * 128],
                                              in_=SS[j][:, c * 128:(c + 1) * 128], identity=IDB[:])
                        return ins
                    S.op("pe", trss, reads=[bSS[j], bCONST], writes=[bPS[mbank[j]]])
                    S.op("act", lambda e, j=j, pb=pb: e.copy(
                        out=STt[:, :, j * 128:(j + 1) * 128],
                        in_=pb[:, 0:512].rearrange("p (c n) -> p c n", c=4)),
                         reads=[bPS[mbank[j]]], writes=[bST])

            def partConv(gi, t0, n, is_s):
                N = n * 128
                for c in range(4):
                    def cv(e, c=c):
                        if is_s:
                            for s_ in range(2):
                                for jt in range(KW):
                                    for i in range(4):
                                        p0 = 32 * i
                                        ins = e.matmul(PS[p0:p0 + 32, c, s_ * 64:(s_ + 1) * 64],
                                                       lhsT=D32[p0:p0 + 32, jt, c, :], rhs=AS[p0:p0 + 32, c, s_, jt:jt + 64],
                                                       start=(jt == 0), stop=(jt == KW - 1), tile_position=(p0, p0))
                            return ins
                        for jt in range(KW):
                            for i in range(4):
                                p0 = 32 * i
                                ins = e.matmul(PS[p0:p0 + 32, c, 0:N], lhsT=D32[p0:p0 + 32, jt, c, :],
                                               rhs=ABF[p0:p0 + 32, c, jt:jt + N],
                                               start=(jt == 0), stop=(jt == KW - 1), tile_position=(p0, p0))
                        return ins
                    S.op("pe", cv, reads=[bABUF, bPAR], writes=[bPS[c]])
                    S.op("act", lambda e, c=c: e.activation(out=ACC[:, c, 0:N], in_=PS[:, c, 0:N], func=AF.Identity,
                                                            bias=CB[:, c:c + 1], scale=1.0),
                         reads=[bPS[c], bPAR], writes=[bACC[c]])

            def partStats(gi, t0, n, is_s):
                N = n * 128
                sqbufs = [(SQ[0], bSQ[0]), (SIL[0], bSIL[0]), (SIL[1], bSIL[1])]
                for c in range(4):
                    sq, bsq = sqbufs[c % 3]
                    S.op("act", lambda e, c=c, sq=sq: e.activation(out=sq[:, 0:N], in_=ACC[:, c, 0:N], func=AF.Square),
                         reads=[bACC[c]], writes=[bsq])
                    S.op("pe", lambda e, c=c: e.matmul(PS[:, 4, 0:N], lhsT=ONESC[:], rhs=ACC[:, c, 0:N],
                                                       start=(c == 0), stop=(c == 3)),
                         reads=[bACC[c], bCONST], writes=[bPS[4]])
                    S.op("pe", lambda e, c=c, sq=sq: e.matmul(PS[:, 5, 0:N], lhsT=ONESC[:], rhs=sq[:, 0:N],
                                                              start=(c == 0), stop=(c == 3)),
                         reads=[bsq, bCONST], writes=[bPS[5]])
                S.op("act", lambda e: e.copy(out=MEANB[:, 0:N], in_=PS[:, 4, 0:N]), reads=[bPS[4]], writes=[bGT[0]])
                S.op("act", lambda e: e.activation(out=RSTDB[:, 0:N], in_=PS[:, 4, 0:N], func=AF.Square),
                     reads=[bPS[4]], writes=[bGT[1]])
                S.op("dve", lambda e: e.tensor_tensor(out=RSTDB[:, 0:N], in0=PS[:, 5, 0:N], in1=RSTDB[:, 0:N],
                                                      op=ALU.subtract),
                     reads=[bPS[5], bGT[1]], writes=[bGT[1]])
                S.op("act", lambda e: e.activation(out=RSTDB[:, 0:N], in_=RSTDB[:, 0:N], func=AF.Sqrt,
                                                   bias=EPS1[:, 0:1], scale=1.0),
                     reads=[bGT[1], bCONST], writes=[bGT[1]])
                S.op("dve", lambda e: e.reciprocal(out=RSTDB[:, 0:N], in_=RSTDB[:, 0:N]), reads=[bGT[1]], writes=[bGT[1]])

            def partNorm(gi, t0, n, is_s, chunks=range(4)):
                N = n * 128
                for c in chunks:
                    S.op("dve", lambda e, c=c: e.tensor_tensor(out=ACC[:, c, 0:N], in0=ACC[:, c, 0:N],
                                                               in1=MEANB[:, 0:N], op=ALU.subtract),
                         reads=[bACC[c], bGT[0], bGT[1]], writes=[bACC[c]])
                    S.op("dve", lambda e, c=c: e.tensor_tensor(out=ACC[:, c, 0:N], in0=ACC[:, c, 0:N],
                                                               in1=RSTDB[:, 0:N], op=ALU.mult),
                         reads=[bACC[c], bGT[0], bGT[1]], writes=[bACC[c]])
                    S.op("act", lambda e, c=c: e.activation(out=CT[:, c, 0:N], in_=ACC[:, c, 0:N], func=AF.Silu,
                                                            scale=CLG[:, c:c + 1], bias=CLB[:, c:c + 1]),
                         reads=[bACC[c], bPAR], writes=[bCT])

            def partOut(gi, t0, n, is_s):
                for j in range(n):
                    t = t0 + j
                    mb = 4 + 2 * (rr["mx"] % 2)
                    rr["mx"] += 1

                    def mmo(e, j=j, mb=mb):
                        for half in range(2):
                            Wo = WoutA if half == 0 else WoutB
                            for k in range(8):
                                lhs = CT[:, k, j * 128:(j + 1) * 128] if k < 4 else STt[:, k - 4, j * 128:(j + 1) * 128]
                                ins = e.matmul(PS[:, mb + half, :], lhsT=lhs, rhs=Wo[:, k, :],
                                               start=(k == 0), stop=(k == 7))
                        return ins
                    S.op("pe", mmo, reads=[bRING[sA], bRING[sB], bCT, bST], writes=bb(mb) + bb(mb + 1))
                    msrc = PS[:, mb:mb + 2, :].rearrange("p a n -> p (a n)")
                    S.op("dve", lambda e, t=t, msrc=msrc: e.scalar_tensor_tensor(
                        out=X[:, t, :], in0=X[:, t, :], scalar=ALPHA, in1=msrc, op0=ALU.mult, op1=ALU.add),
                         reads=[bX[t]] + bb(mb) + bb(mb + 1), writes=[bX[t]])
                layer_norm_group([t0 + j for j in range(n)], EPS1)

            partA(0, *groups[0])
            for gi, g in enumerate(groups):
                partB(gi, *g)
                if gi + 1 < len(groups):
                    ensure_xt([groups[gi + 1][0] + j for j in range(groups[gi + 1][1])])
                partConv(gi, *g)
                partSGU(gi, *g)
                partStats(gi, *g)
                if gi + 1 < len(groups):
                    partA(gi + 1, *groups[gi + 1], between=lambda c, gi=gi, g=g: partNorm(gi, *g, chunks=[c]))
                else:
                    partNorm(gi, *g)
                partOut(gi, *g)
                if gi >= 1:
                    ensure_xt([groups[gi - 1][0] + j for j in range(groups[gi - 1][1])])
            load_next()
            load_next()

        for _ in range(NSLOT):
            load_next()
        prep_dma(0)
        for b in range(NB):
            for t in range(NT):
                S.dma("sp", X[:, t, :], xin[(b * NT + t) * 128:(b * NT + t + 1) * 128, :], writes=[bX[t]],
                      sem="xin%d" % (t // 3))
            for t in range(NT):
                i = rr["ln"] % 2
                rr["ln"] += 1
                S.op("act", lambda e, t=t, i=i: e.copy(out=XN[i][:], in_=X[:, t, :]), reads=[bX[t]], writes=[bXN[i]])
                transposes_to_XT(t, i, 0 if t % 2 == 0 else 2)
            for l in range(NL):
                prep_pe(l)
                if dbg_stage == "x0" and l == 0:
                    for t in range(NT):
                        S.dma("sp", dbg_d[(b * NT + t) * 128:(b * NT + t + 1) * 128, :], X[:, t, :], reads=[bX[t]], sem="sout")
                ffn(l, 1, l == NL - 1)
                if dbg_stage == "x1" and l == 0:
                    for t in range(NT):
                        S.dma("sp", dbg_d[(b * NT + t) * 128:(b * NT + t + 1) * 128, :], X[:, t, :], reads=[bX[t]], sem="sout")
                mixer(l, b)
                if dbg_stage == "x2" and l == 0:
                    for t in range(NT):
                        S.dma("sp", dbg_d[(b * NT + t) * 128:(b * NT + t + 1) * 128, :], X[:, t, :], reads=[bX[t]], sem="sout")
                if l + 1 < NL:
                    prep_dma(l + 1)
                elif b + 1 < NB:
                    prep_dma(0)
                ffn(l, 2, l == NL - 1)
            for t in range(NT):
                gt = b * NT + t
                if gt == 0:
                    continue
                S.dma("sp", y_d[(gt - 1) * 128:gt * 128, :], X[:, t, :], reads=[bX[t]], sem="yout")
        S.wait_all("sp", ["yout", "sout"])
        block = es.enter_context(nc.Block())
        S.emit(block)
    return nc


_WNAMES = ["w_ffn1_gate", "w_ffn1_up", "w_ffn1_down", "w_ffn2_gate", "w_ffn2_up", "w_ffn2_down", "w_in", "w_out",
           "ln1_g", "ln1_b", "ln2_g", "ln2_b", "ln3_g", "ln3_b", "conv_k", "conv_b", "conv_ln_g", "conv_ln_b",
           "sgu_ln_g", "sgu_ln_b", "w_sgu", "b_sgu"]


def make_in_maps(inputs, cores=range(NCORES)):
    xp = np.asarray(inputs["x_prompt"], dtype=np.float32)
    xs = np.asarray(inputs["x_sample"], dtype=np.float32)
    cc = np.asarray(inputs["cache_conv"], dtype=np.float32)
    W = {k: np.ascontiguousarray(np.asarray(inputs[k], dtype=np.float32)) for k in _WNAMES}
    maps = []
    for c in cores:
        seq, part = c // 4, c % 4
        s0 = part * OWN
        xin = np.zeros((NB * NT * 128, D), np.float32)
        if part > 0:
            xin[0:128] = xp[seq, s0 - 128:s0]
        xin[128:128 + OWN] = xp[seq, s0:s0 + OWN]
        xin[128 + OWN:] = xs[2 * c:2 * c + 2].reshape(128, D)
        m = {"xin": xin,
             "cmask": np.full((128, 1), 0.0 if part == 0 else 1.0, np.float32),
             "cache": np.ascontiguousarray(cc[:, 2 * c:2 * c + 2])}
        m.update(W)
        maps.append(m)
    return maps


_NC_CACHE = {}


def kernel(**inputs):
    if "nc" not in _NC_CACHE:
        _NC_CACHE["nc"] = build_program(DEPTH)
    nc = _NC_CACHE["nc"]
    maps = make_in_maps(inputs)
    res = run_bass_kernel_spmd(nc, maps, core_ids=list(range(NCORES)))
    R = res.results
    y_prompt = np.zeros((2, 8192, D), np.float32)
    y_sample = np.zeros((16, 64, D), np.float32)
    conv_p = np.zeros((DEPTH, 2, CTXN, DC), np.float32)
    conv_s = np.zeros((DEPTH, 16, CTXN, DC), np.float32)
    sgu_v = np.zeros((DEPTH, 16, 64, DC), np.float32)
    for c in range(NCORES):
        seq, part = c // 4, c % 4
        y = np.asarray(R[c]["y"])
        y_prompt[seq, part * OWN:(part + 1) * OWN] = y[0:OWN]
        y_sample[2 * c:2 * c + 2] = y[OWN:].reshape(2, 64, D)
        if part == 3:
            conv_p[:, seq] = np.asarray(R[c]["conv_p"])
        conv_s[:, 2 * c:2 * c + 2] = np.asarray(R[c]["conv_s"])
        sgu_v[:, 2 * c:2 * c + 2] = np.asarray(R[c]["sgu_v"]).reshape(DEPTH, 2, 64, DC)
    return (y_prompt, y_sample, conv_p, conv_s, sgu_v)
```
